# Optimizing a Trainium2 kernel written in Bass

```python
import math
import jax, jax.numpy as jnp
from jax import lax
import numpy as np

D_MODEL = 1024
BATCH = 8
SEQ = 4096
DEPTH = 2

N_MIXERS = 2
GRID_W = 64
NA_HEADS = 16
NA_HEAD_DIM = D_MODEL // NA_HEADS
NA_WIN_ROWS = 8
NA_WIN_COLS = 16
FN_GROUPS = 8
FN_GROUP_DIM = D_MODEL // FN_GROUPS
D_FF = ((8 * D_MODEL // 3 + 127) // 128) * 128
CONV_W = 3
LN_EPS = 1e-5
ALPHA = (2.0 * DEPTH) ** 0.25
BETA = (8.0 * DEPTH) ** -0.25
N_NA_LAYERS = (DEPTH + 1) // 2
N_FN_LAYERS = DEPTH // 2

kernel_name = "hybrid_natten_fnet_convffn_deepnorm_adaln"


def layer_norm(x, g, b):
    xf = x.astype(jnp.float32)
    mu = jnp.mean(xf, axis=-1, keepdims=True)
    var = jnp.mean(jnp.square(xf - mu), axis=-1, keepdims=True)
    y = (xf - mu) * lax.rsqrt(var + LN_EPS)
    return (y * g.astype(jnp.float32) + b.astype(jnp.float32)).astype(x.dtype)


def neighborhood_attention(u, w_qkv, rpb, w_o):
    B, S, D = u.shape
    rows = S // GRID_W
    kr = min(NA_WIN_ROWS, rows)
    qkv = jnp.einsum('bsd,de->bse', u, w_qkv)
    q, k, v = jnp.split(qkv, 3, axis=-1)
    grid = lambda t: t.reshape(B, rows, GRID_W, NA_HEADS, NA_HEAD_DIM)
    q = grid(q) * (NA_HEAD_DIM ** -0.5)
    k = grid(k)
    v = grid(v)
    qcol = np.arange(GRID_W)
    col_start = np.clip(qcol - NA_WIN_COLS // 2, 0, GRID_W - NA_WIN_COLS)
    kcol = np.arange(GRID_W)
    col_in = (kcol[None, :] >= col_start[:, None]) & (kcol[None, :] < col_start[:, None] + NA_WIN_COLS)
    dc_idx = np.clip(kcol[None, :] - qcol[:, None] + NA_WIN_COLS - 1, 0, 2 * NA_WIN_COLS - 2)
    row_start = np.clip(np.arange(rows) - kr // 2, 0, rows - kr)
    col_mask = jnp.asarray(col_in)[None, None, :, None, :]

    def one_row(args):
        r, rs = args
        q_r = lax.dynamic_index_in_dim(q, r, axis=1, keepdims=False)
        k_b = lax.dynamic_slice_in_dim(k, rs, kr, axis=1)
        v_b = lax.dynamic_slice_in_dim(v, rs, kr, axis=1)
        s = jnp.einsum('bqhd,bikhd->bhqik', q_r, k_b, preferred_element_type=jnp.float32)
        dr_idx = rs + jnp.arange(kr) - r + NA_WIN_ROWS - 1
        bias = rpb[:, dr_idx[None, :, None], dc_idx[:, None, :]]
        s = s + bias[None].astype(jnp.float32)
        s = jnp.where(col_mask, s, -jnp.inf)
        p = jax.nn.softmax(s.reshape(B, NA_HEADS, GRID_W, kr * GRID_W), axis=-1)
        p = p.reshape(B, NA_HEADS, GRID_W, kr, GRID_W).astype(v_b.dtype)
        return jnp.einsum('bhqik,bikhd->bqhd', p, v_b)

    out = lax.map(one_row, (jnp.arange(rows, dtype=jnp.int32), jnp.asarray(row_start, dtype=jnp.int32)))
    out = jnp.transpose(out, (1, 0, 2, 3, 4)).reshape(B, S, D)
    return jnp.einsum('bsd,de->bse', out, w_o)


def fourier_mix(u, w_o):
    B, S, D = u.shape
    ug = u.astype(jnp.float32).reshape(B, S, FN_GROUPS, FN_GROUP_DIM)
    y = jnp.fft.fftn(ug, axes=(1, 3), norm='ortho').real
    y = y.reshape(B, S, D).astype(u.dtype)
    return jnp.einsum('bsd,de->bse', y, w_o)


def conv_ffn(u, w_up, conv_w, conv_b, w_down):
    S = u.shape[1]
    a, g = jnp.split(jnp.einsum('bsd,df->bsf', u, w_up), 2, axis=-1)
    half = CONV_W // 2
    ap = jnp.pad(a, ((0, 0), (half, half), (0, 0)))
    a = conv_b + sum(ap[:, j:j + S] * conv_w[j] for j in range(CONV_W))
    h = jax.nn.gelu(a) * g
    return jnp.einsum('bsf,fd->bsd', h, w_down)


def setup_inputs(seed: int = 0) -> dict:
    key = jax.random.key(seed)
    ks = jax.random.split(key, 20)
    D, F = D_MODEL, D_FF
    nrm = lambda k, shape, s: jax.random.normal(k, shape, jnp.float32) * s
    x = nrm(ks[0], (BATCH, SEQ, D), 1.0)
    c = nrm(ks[1], (BATCH, D), 1.0)
    ada_w = nrm(ks[2], (DEPTH, D, 6 * D), 0.1 * D ** -0.5)
    ada_b = nrm(ks[3], (DEPTH, 6 * D), 0.01)
    w_qk = nrm(ks[4], (N_NA_LAYERS, D, 2 * D), D ** -0.5)
    w_v = nrm(ks[5], (N_NA_LAYERS, D, D), BETA * D ** -0.5)
    na_w_qkv = jnp.concatenate([w_qk, w_v], axis=-1)
    na_rpb = nrm(ks[6], (N_NA_LAYERS, NA_HEADS, 2 * NA_WIN_ROWS - 1, 2 * NA_WIN_COLS - 1), 0.02)
    na_w_o = nrm(ks[7], (N_NA_LAYERS, D, D), BETA * D ** -0.5)
    fn_w_o = nrm(ks[8], (N_FN_LAYERS, D, D), BETA * D ** -0.5)
    ln1_g = 1.0 + nrm(ks[9], (DEPTH, D), 0.01)
    ln1_b = nrm(ks[10], (DEPTH, D), 0.01)
    ffn_w_up = nrm(ks[11], (DEPTH, D, 2 * F), BETA * D ** -0.5)
    ffn_conv_w = nrm(ks[12], (DEPTH, CONV_W, F), CONV_W ** -0.5)
    ffn_conv_b = nrm(ks[13], (DEPTH, F), 0.01)
    ffn_w_down = nrm(ks[14], (DEPTH, F, D), BETA * F ** -0.5)
    ln2_g = 1.0 + nrm(ks[15], (DEPTH, D), 0.01)
    ln2_b = nrm(ks[16], (DEPTH, D), 0.01)
    return {"x": x, "c": c, "ada_w": ada_w, "ada_b": ada_b,
            "na_w_qkv": na_w_qkv, "na_rpb": na_rpb, "na_w_o": na_w_o, "fn_w_o": fn_w_o,
            "ln1_g": ln1_g, "ln1_b": ln1_b,
            "ffn_w_up": ffn_w_up, "ffn_conv_w": ffn_conv_w, "ffn_conv_b": ffn_conv_b, "ffn_w_down": ffn_w_down,
            "ln2_g": ln2_g, "ln2_b": ln2_b}


def reference(x, c, ada_w, ada_b, na_w_qkv, na_rpb, na_w_o, fn_w_o, ln1_g, ln1_b,
              ffn_w_up, ffn_conv_w, ffn_conv_b, ffn_w_down, ln2_g, ln2_b):
    cs = jax.nn.silu(c)
    for i in range(DEPTH):
        mod = jnp.einsum('bd,de->be', cs, ada_w[i]) + ada_b[i]
        sh1, sc1, g1, sh2, sc2, g2 = jnp.split(mod[:, None, :], 6, axis=-1)
        u = x * (1.0 + sc1) + sh1
        j = i // N_MIXERS
        if i % N_MIXERS == 0:
            y = neighborhood_attention(u, na_w_qkv[j], na_rpb[j], na_w_o[j])
        else:
            y = fourier_mix(u, fn_w_o[j])
        x = layer_norm(ALPHA * x + (1.0 + g1) * y, ln1_g[i], ln1_b[i])
        u = x * (1.0 + sc2) + sh2
        y = conv_ffn(u, ffn_w_up[i], ffn_conv_w[i], ffn_conv_b[i], ffn_w_down[i])
        x = layer_norm(ALPHA * x + (1.0 + g2) * y, ln2_g[i], ln2_b[i])
    return x
```

```python
import math
from contextlib import ExitStack

import numpy as np
import ml_dtypes

import concourse.bass as bass
import concourse.mybir as mybir
from concourse.bass_utils import run_bass_kernel_spmd

F32 = mybir.dt.float32
BF16 = mybir.dt.bfloat16
AF = mybir.ActivationFunctionType
ALU = mybir.AluOpType

D = 1024
S = 4096
NCH = 8
DFF = 2816
NF = 22
NH = 16
ALPHA = math.sqrt(2.0)
EPS_P = 1e-5 / (ALPHA * ALPHA)
NEG = -30000.0
NSLOT = 16

ENGS = ("pe", "act", "dve", "pool", "sp")


class Buf:
    __slots__ = ("name", "last_w", "readers", "sem", "dcount")

    def __init__(self, name):
        self.name = name
        self.last_w = None
        self.readers = []
        self.sem = None
        self.dcount = 0


class _Rec:
    def __init__(self):
        self.call = None

    def __getattr__(self, name):
        def f(*a, **kw):
            assert self.call is None
            self.call = (name, a, kw)
            return self
        return f


def _record(fn):
    r = _Rec()
    fn(r)
    assert r.call is not None
    return r.call


class Prog:
    def __init__(self, nc, ctx):
        self.nc = nc
        self.ctx = ctx
        self.q = {e: [] for e in ENGS}
        self.seq = {e: 0 for e in ENGS}
        self.esem = {e: ctx.enter_context(nc.semaphore("es_" + e)) for e in ENGS}
        self.waited = {}
        self.dma_sems = []
        self.dma_bufs = []
        self.free_sems = []

    def buf(self, name="b"):
        return Buf(name)

    def bufs(self, n, name="b"):
        return [Buf("%s%d" % (name, i)) for i in range(n)]

    def _ensure_sem(self, b):
        if b.sem is None:
            b.sem = self.ctx.enter_context(self.nc.semaphore("ds_%d" % len(self.dma_bufs)))
            self.dma_bufs.append(b)

    def _wait(self, eng, tok):
        if tok is None:
            return
        if tok[0] == "eng":
            X, n = tok[1], tok[2]
            if X == eng and eng in ("pe", "sp"):
                return
            key = (eng, X)
            if self.waited.get(key, 0) >= n:
                return
            self.waited[key] = n
            self.q[eng].append(("wait", self.esem[X], n))
        else:
            b, cnt = tok[1], tok[2]
            key = (eng, id(b))
            if self.waited.get(key, 0) >= cnt:
                return
            self.waited[key] = cnt
            self.q[eng].append(("wait", b.sem, 16 * cnt))

    def _deps(self, eng, reads, writes):
        best = {}

        def add(t):
            if t is None:
                return
            key = (t[0], t[1] if t[0] == "eng" else id(t[1]))
            if key not in best or best[key][2] < t[2]:
                best[key] = t
        for b in reads:
            add(b.last_w)
        for b in writes:
            add(b.last_w)
            for t in b.readers:
                add(t)
        for t in best.values():
            self._wait(eng, t)

    def _mark(self, tok, reads, writes):
        for b in writes:
            b.last_w = tok
            b.readers = []
        for b in reads:
            b.readers.append(tok)

    def op(self, eng, fn, reads=(), writes=()):
        self._deps(eng, reads, writes)
        self.seq[eng] += 1
        self.q[eng].append(("op", _record(fn)))
        self._mark(("eng", eng, self.seq[eng]), reads, writes)

    def dma(self, eng, fn, reads=(), writes=(), sem_buf=None):
        self._deps(eng, reads, writes)
        self._ensure_sem(sem_buf)
        sem_buf.dcount += 1
        self.q[eng].append(("dma", _record(fn), sem_buf.sem))
        self._mark(("dma", sem_buf, sem_buf.dcount), reads, writes)

    def dma_group(self, eng, fns, reads=(), writes=(), sem_buf=None):
        self._deps(eng, reads, writes)
        self._ensure_sem(sem_buf)
        for fn in fns:
            sem_buf.dcount += 1
            self.q[eng].append(("dma", _record(fn), sem_buf.sem))
        self._mark(("dma", sem_buf, sem_buf.dcount), reads, writes)

    def barrier(self):
        for e in ENGS:
            for x in ENGS:
                if x != e and self.seq[x] > 0:
                    self._wait(e, ("eng", x, self.seq[x]))
            for b in self.dma_bufs:
                if b.dcount > 0:
                    self._wait(e, ("dma", b, b.dcount))

    def emit(self):
        nc = self.nc
        engobj = {"pe": nc.tensor, "act": nc.scalar, "dve": nc.vector, "pool": nc.gpsimd, "sp": nc.sync}
        with nc.Block() as block:
            def run(ename):
                def f(_e):
                    e = engobj[ename]
                    sem = self.esem[ename]
                    for item in self.q[ename]:
                        if item[0] == "wait":
                            e.wait_ge(item[1], item[2])
                        elif item[0] == "op":
                            name, a, kw = item[1]
                            getattr(e, name)(*a, **kw).then_inc(sem, 1)
                        else:
                            name, a, kw = item[1]
                            getattr(e, name)(*a, **kw).then_inc(item[2], 16)
                return f
            reg = {"pe": block.tensor, "act": block.scalar, "dve": block.vector, "pool": block.gpsimd, "sp": block.sync}
            for en in ENGS:
                if self.q[en]:
                    reg[en](run(en))


def _bf(a):
    return np.ascontiguousarray(a.astype(np.float32)).astype(ml_dtypes.bfloat16)


def make_dft_tables():
    ch = np.arange(128)[:, None].astype(np.float64)
    m = np.arange(128)[None, :].astype(np.float64)
    ang = 2 * np.pi * ch * m / 128.0
    sc = 1.0 / math.sqrt(128.0)
    tabC = np.concatenate([np.cos(ang) * sc, -np.sin(ang) * sc], axis=1)
    r = np.arange(64)[:, None].astype(np.float64)
    ka = np.arange(64)[None, :].astype(np.float64)
    th = 2 * np.pi * r * ka / 64.0
    c, s = np.cos(th) / 8.0, np.sin(th) / 8.0
    R_re = np.concatenate([c, s], axis=0)
    R_im = np.concatenate([-s, c], axis=0)
    tabR = np.concatenate([R_re, R_im], axis=1)
    cc = np.arange(64)[:, None, None].astype(np.float64)
    kaa = np.arange(64)[None, :, None].astype(np.float64)
    kb = np.arange(64)[None, None, :].astype(np.float64)
    phi = 2 * np.pi * (cc * kb / 64.0 + cc * kaa / 4096.0)
    V = np.concatenate([np.cos(phi) / 8.0, np.sin(phi) / 8.0], axis=0)
    return _bf(tabC), _bf(tabR), _bf(V.reshape(128, 64 * 64))


def na_row_range(j):
    qlo = 0 if j <= 3 else 2 * j - 3
    qhi = 63 if j >= 28 else 2 * j + 5
    return qlo, qhi


def na_slot(j, r):
    t = r - 2 * j + 7
    if t == 4 and (j, r) in ((2, 1), (3, 3)):
        return 14
    if t == 12 and (j, r) in ((28, 61), (29, 63)):
        return 15
    assert 1 <= t <= 14
    return t - 1


def make_bias_table(rpb):
    kc = np.arange(64)[:, None]
    qc = np.arange(64)[None, :]
    cs = np.clip(qc - 8, 0, 48)
    col_in = (kc >= cs) & (kc < cs + 16)
    dc = np.clip(kc - qc + 15, 0, 30)
    tab = np.full((NH, 2, 64, NSLOT, 64), NEG, np.float32)
    for slot in range(NSLOT):
        t = slot + 1 if slot < 14 else (4 if slot == 14 else 12)
        for krb in range(2):
            dr = krb + 7 - t
            if abs(dr) > 7:
                continue
            if slot < 14 and ((krb == 1 and t == 4) or (krb == 0 and t == 12)):
                continue
            g = rpb[:, dr + 7, :][:, dc]
            tab[:, krb, :, slot, :] = np.where(col_in[None], g, np.float32(NEG))
    return tab.reshape(NH, 128, NSLOT * 64)


def _check_na_tables():
    rs = np.clip(np.arange(64) - 4, 0, 56)
    cover = np.zeros((64, 64), np.int32)
    for j in range(32):
        qlo, qhi = na_row_range(j)
        for r in range(qlo, qhi + 1):
            slot = na_slot(j, r)
            t = slot + 1 if slot < 14 else (4 if slot == 14 else 12)
            assert t == r - 2 * j + 7
            for krb in range(2):
                kr = 2 * j + krb
                valid_tab = not (slot < 14 and ((krb == 1 and t == 4) or (krb == 0 and t == 12)))
                valid = rs[r] <= kr <= rs[r] + 7
                assert valid == valid_tab, (j, r, krb)
                if valid:
                    cover[r, kr] += 1
    for r in range(64):
        assert cover[r].sum() == 8 and cover[r, rs[r]:rs[r] + 8].sum() == 8


class Builder:
    def __init__(self, phases=("ada", "attn", "ffn0", "four", "ffn1"), debug=False):
        self.phases = phases
        self.debug = debug
        self.nc = bass.Bass("TRN2", target_bir_lowering=False)
        self.dbg_outs = {}
        self.deferred = []
        self.bg_ada = []
        self.bg_pre = []

    def dram_in(self, name, shape, dt=F32):
        return self.nc.dram_tensor(name, list(shape), dt, kind="ExternalInput").ap()

    def build(self):
        nc = self.nc
        self.xT = self.dram_in("xT", [128, NCH, S])
        self.ccol = self.dram_in("ccol", [128, NCH])
        self.ada_w = self.dram_in("ada_w", [2, 128, NCH, 6 * D])
        self.ada_b = self.dram_in("ada_b", [2, 6 * D])
        self.wqkv = self.dram_in("wqkv", [NCH, 128, NCH * 3 * 128])
        self.tab = self.dram_in("tab", [NH, 128, NSLOT * 64])
        self.na_wo = self.dram_in("na_wo", [128, NCH, D])
        self.fn_wo = self.dram_in("fn_wo", [128, NCH, D])
        self.lnp = self.dram_in("lnp", [128, 4, 2, NCH])
        self.w_up = self.dram_in("w_up", [2, 128, NCH, 2 * DFF])
        self.w_dn = self.dram_in("w_dn", [2, 128, NF, D])
        self.convp = self.dram_in("convp", [2, 128, 4, NF])
        self.tabC = self.dram_in("tabC", [128, 256], BF16)
        self.tabR = self.dram_in("tabR", [128, 128], BF16)
        self.tabV = self.dram_in("tabV", [128, 4096], BF16)
        self.ident = self.dram_in("ident", [128, 128], BF16)
        self.out = nc.dram_tensor("outT", [128, NCH, S], F32, kind="ExternalOutput").ap()
        self.xs = [nc.dram_tensor("xs%d" % i, [128, NCH, S], F32, kind="Internal").ap() for i in range(3)]
        self.wup_bf = [nc.dram_tensor("wupbf%d" % i, [128, NCH, 2 * DFF], BF16, kind="Internal").ap() for i in range(2)]
        self.wdn_bf = [nc.dram_tensor("wdnbf%d" % i, [128, NF, D], BF16, kind="Internal").ap() for i in range(2)]
        self.wo_bf = [nc.dram_tensor("wobf%d" % i, [128, NCH, D], BF16, kind="Internal").ap() for i in range(2)]

        with ExitStack() as ctx:
            self.ctx = ctx
            self.P = P = Prog(nc, ctx)
            self.ps = ctx.enter_context(nc.psum_tensor("ps", [128, 4096], F32))
            self.pb = P.bufs(8, "psb")
            self.modp = self.sb(ctx, "modp", [128, 2, 48], F32)
            self.lnp_sb = self.sb(ctx, "lnp_sb", [128, 4 * 2 * NCH], F32)
            self.convp_sb = self.sb(ctx, "convp_sb", [128, 2, 4 * NF], F32)
            self.ones_bf = self.sb(ctx, "ones_bf", [128, 128], BF16)
            self.ident_bf = self.sb(ctx, "ident_bf", [128, 128], BF16)
            self.b_const = P.buf("const")
            self.b_modg = [P.bufs(6, "mod%d_" % l) for l in range(2)]
            self.b_xs = [P.bufs(8, "xs%d_" % i) for i in range(3)]
            self.b_out = P.bufs(16, "out")
            self.b_xin = P.bufs(8, "xin")

            P.dma("sp", lambda e: e.dma_start(out=self.lnp_sb[:], in_=self.lnp.rearrange("p a b c -> p (a b c)")),
                  writes=[self.b_const], sem_buf=self.b_const)
            for l in range(2):
                P.dma("sp", (lambda l: lambda e: e.dma_start(out=self.convp_sb[:, l, :], in_=self.convp[l].rearrange("p a f -> p (a f)")))(l),
                      writes=[self.b_const], sem_buf=self.b_const)
            P.dma("sp", lambda e: e.dma_start(out=self.ident_bf[:], in_=self.ident[:, :]), writes=[self.b_const], sem_buf=self.b_const)
            P.op("pool", lambda e: e.memset(self.ones_bf[:], 1.0), writes=[self.b_const])

            self.b_wupbf, self.b_wdnbf, self.b_wobf = P.bufs(2, "wupbf"), P.bufs(2, "wdnbf"), P.bufs(2, "wobf")
            self.make_precast()
            src, b_src = self.xT, self.b_xin
            with ExitStack() as actx:
                self.phase_ada(actx)
                if "attn" in self.phases:
                    dst = self.xs[0] if self.phases[-1] != "attn" else self.out
                    b_dst = self.b_xs[0] if self.phases[-1] != "attn" else self.b_out
                    self.phase_attn(src, b_src, dst, b_dst)
                    src, b_src = dst, b_dst
                self.drain_bg()
                P.barrier()
            if "ffn0" in self.phases:
                dst = self.xs[1] if self.phases[-1] != "ffn0" else self.out
                b_dst = self.b_xs[1] if self.phases[-1] != "ffn0" else self.b_out
                self.phase_ffn(0, src, b_src, dst, b_dst)
                src, b_src = dst, b_dst
            if "four" in self.phases:
                dst = self.xs[2] if self.phases[-1] != "four" else self.out
                b_dst = self.b_xs[2] if self.phases[-1] != "four" else self.b_out
                self.phase_four(src, b_src, dst, b_dst)
                src, b_src = dst, b_dst
            if "ffn1" in self.phases:
                self.phase_ffn(1, src, b_src, self.out, self.b_out)
            P.barrier()
            P.emit()
        return nc

    def sb(self, ctx, name, shape, dt):
        self._uid = getattr(self, "_uid", 0) + 1
        return ctx.enter_context(self.nc.sbuf_tensor("%s_%d" % (name, self._uid), list(shape), dt))

    def bank(self, b, c0=0, c1=512, p0=0, p1=128):
        return self.ps[p0:p1, b * 512 + c0: b * 512 + c1]

    def mod(self, l, which, k):
        base = {"sh1": 0, "sc1": 8, "g1": 16, "sh2": 24, "sc2": 32, "g2": 40}[which]
        return self.modp[:, l, base + k: base + k + 1]

    def bmod(self, l, which):
        return self.b_modg[l][{"sh1": 0, "sc1": 1, "g1": 2, "sh2": 3, "sc2": 4, "g2": 5}[which]]

    def lnv(self, idx, gb, k):
        o = (idx * 2 + gb) * NCH + k
        return self.lnp_sb[:, o:o + 1]

    def phase_ada(self, ctx):
        P, nc = self.P, self.nc
        c_sb = self.sb(ctx, "c_sb", [128, NCH], F32)
        cs_bf = self.sb(ctx, "cs_bf", [128, NCH], BF16)
        one_f = self.sb(ctx, "one_f", [1, 1], F32)
        stage = [self.sb(ctx, "ada_st%d" % i, [128, NCH, 256], BF16) for i in range(2)]
        brow = [self.sb(ctx, "ada_br%d" % i, [1, 256], F32) for i in range(2)]
        mrow = [self.sb(ctx, "ada_mr%d" % i, [1, 256], F32) for i in range(2)]
        b_c, b_cs = P.buf(), P.buf()
        b_st, b_br, b_mr = P.bufs(2, "ast"), P.bufs(2, "abr"), P.bufs(2, "amr")
        P.dma("sp", lambda e: e.dma_start(out=c_sb[:], in_=self.ccol[:, :]), writes=[b_c], sem_buf=b_c)
        P.op("act", lambda e: e.activation(out=cs_bf[:], in_=c_sb[:], func=AF.Silu), reads=[b_c], writes=[b_cs])
        P.op("pool", lambda e: e.memset(one_f[:], 1.0), writes=[b_cs])
        RB, TBK = 0, 1

        def dma(n):
            l, pc, slot = n // 24, n % 24, n % 2
            P.dma("pool", lambda e: e.dma_start(out=stage[slot][:], in_=self.ada_w[l, :, :, pc * 256:(pc + 1) * 256]),
                  writes=[b_st[slot]], sem_buf=b_st[slot])
            P.dma("sp", lambda e: e.dma_start(out=brow[slot][:], in_=self.ada_b[l:l + 1, pc * 256:(pc + 1) * 256]),
                  writes=[b_br[slot]], sem_buf=b_br[slot])

        pend = {"n": None}

        def trans(n):
            l, pc, slot = n // 24, n % 24, n % 2
            for jj in range(2):
                P.op("pe", lambda e: e.matmul(self.bank(TBK, jj, jj + 1), lhsT=mrow[slot][0:1, jj * 128:(jj + 1) * 128], rhs=one_f[0:1, 0:1],
                                              start=True, stop=True, skip_group_check=True),
                     reads=[b_mr[slot], b_cs], writes=[self.pb[TBK]])
            grp = (pc * 2) // 8
            add, mul = [(0.0, 1.0), (1.0, 1.0), (1.0, 1.0 / ALPHA)][grp % 3]
            P.op("dve", lambda e: e.tensor_scalar(out=self.modp[:, l, pc * 2:(pc + 1) * 2], in0=self.bank(TBK, 0, 2),
                                                  scalar1=add, scalar2=mul, op0=ALU.add, op1=ALU.mult),
                 writes=[self.pb[TBK], self.b_modg[l][grp]])

        def flush():
            if pend["n"] is not None:
                trans(pend["n"])
                pend["n"] = None

        def comp(n):
            l, pc, slot = n // 24, n % 24, n % 2
            for k in range(NCH):
                P.op("pe", lambda e: e.matmul(self.bank(RB, 0, 256, 0, 1), lhsT=cs_bf[:, k:k + 1], rhs=stage[slot][:, k, :],
                                              start=(k == 0), stop=(k == NCH - 1), skip_group_check=True),
                     reads=[b_st[slot], b_cs], writes=[self.pb[RB]])
            P.op("dve", lambda e: e.tensor_tensor(out=mrow[slot][:], in0=self.bank(RB, 0, 256, 0, 1), in1=brow[slot][:], op=ALU.add),
                 reads=[b_br[slot]], writes=[self.pb[RB], b_mr[slot]])
            flush()
            pend["n"] = n

        seq = []
        for n in range(48):
            seq.append(("d", n))
            if n >= 1:
                seq.append(("c", n - 1))
        seq.append(("c", 47))
        ncrit = 8
        for kind, n in seq:
            th = (lambda n=n: dma(n)) if kind == "d" else (lambda n=n: comp(n))
            if (kind == "d" and n < ncrit) or (kind == "c" and n < ncrit):
                th()
                if kind == "c" and n == ncrit - 1:
                    flush()
            else:
                self.bg_ada.append(th)
        self.bg_ada.append(flush)

    def stream_u(self, ph, src, b_src, l, sc, sh, uT, b_u):
        P = self.P
        xin = [self.sb(ph, "su_x%d" % i, [128, NCH, 512], F32) for i in range(3)]
        b_x = P.bufs(3, "su_x")
        self._stream_u_body(src, b_src, l, sc, sh, uT, b_u, xin, b_x)

    def _stream_u_body(self, src, b_src, l, sc, sh, uT, b_u, xin, b_x):
        P = self.P
        for tb in range(8):
            slot = tb % 3
            for hh in range(2):
                P.dma("sp", (lambda tb, slot, hh: lambda e: e.dma_start(
                    out=xin[slot][:, hh * 4:(hh + 1) * 4, :], in_=src[:, hh * 4:(hh + 1) * 4, tb * 512:(tb + 1) * 512]))(tb, slot, hh),
                    reads=[b_src[tb]], writes=[b_x[slot]], sem_buf=b_x[slot])
            for k in range(NCH):
                if k % 2 == 0:
                    P.op("act", (lambda tb, slot, k: lambda e: e.activation(
                        out=uT[:, k, tb * 512:(tb + 1) * 512], in_=xin[slot][:, k, :], func=AF.Identity,
                        scale=self.mod(l, sc, k), bias=self.mod(l, sh, k)))(tb, slot, k),
                        reads=[b_x[slot], self.bmod(l, sc), self.bmod(l, sh)], writes=[b_u[tb]])
                else:
                    P.op("dve", (lambda tb, slot, k: lambda e: e.tensor_scalar(
                        out=uT[:, k, tb * 512:(tb + 1) * 512], in0=xin[slot][:, k, :],
                        scalar1=self.mod(l, sc, k), scalar2=self.mod(l, sh, k), op0=ALU.mult, op1=ALU.add))(tb, slot, k),
                        reads=[b_x[slot], self.bmod(l, sc), self.bmod(l, sh)], writes=[b_u[tb]])

    def ln_pre(self, T, zk, bzk, r, tmp, b_tmp):
        P = self.P
        P.op("act", lambda e: e.activation(out=tmp["zb"][:, r, 0:T], in_=zk, func=AF.Copy), reads=[bzk], writes=[b_tmp["zb"][r]])
        P.op("act", lambda e: e.activation(out=tmp["zsq"][:, r, 0:T], in_=zk, func=AF.Square), reads=[bzk], writes=[b_tmp["zsq"][r]])

    def ln_stats_mm(self, T, k, r, tmp, b_tmp, s1, s2, same_bank):
        P = self.P
        P.op("pe", lambda e: e.matmul(self.bank(s1[0], s1[1], s1[1] + T), lhsT=self.ones_bf[:], rhs=tmp["zb"][:, r, 0:T],
                                      start=(k == 0), stop=(k == NCH - 1), skip_group_check=True),
             reads=[b_tmp["zb"][r], self.b_const], writes=[self.pb[s1[0]]])
        P.op("pe", lambda e: e.matmul(self.bank(s2[0], s2[1], s2[1] + T), lhsT=self.ones_bf[:], rhs=tmp["zsq"][:, r, 0:T],
                                      start=(k == 0 and not same_bank), stop=(k == NCH - 1), skip_group_check=True),
             reads=[b_tmp["zsq"][r], self.b_const], writes=[self.pb[s2[0]]])

    def ln_finish_groups(self, T, z, b_z, ln_idx, tmp, b_tmp, s1, s2, add_eng="dve"):
        P = self.P
        bs = b_tmp["st"]
        S1 = self.bank(s1[0], s1[1], s1[1] + T)
        S2 = self.bank(s2[0], s2[1], s2[1] + T)
        groups = []

        def g0():
            P.op("act", lambda e: e.activation(out=tmp["mean"][:, 0:T], in_=S1, func=AF.Copy, scale=1.0 / D),
                 writes=[self.pb[s1[0]], bs])
            P.op("act", lambda e: e.activation(out=tmp["msq"][:, 0:T], in_=S1, func=AF.Square, scale=1.0 / D),
                 writes=[self.pb[s1[0]], bs])

        def g1():
            P.op("dve", lambda e: e.scalar_tensor_tensor(out=tmp["var"][:, 0:T], in0=S2, scalar=1.0 / D, in1=tmp["msq"][:, 0:T],
                                                         op0=ALU.mult, op1=ALU.subtract),
                 reads=[bs], writes=[self.pb[s2[0]], bs])
            P.op("dve", lambda e: e.tensor_scalar(out=tmp["var"][:, 0:T], in0=tmp["var"][:, 0:T], scalar1=EPS_P, scalar2=None, op0=ALU.add),
                 reads=[bs], writes=[bs])

        def g2():
            P.op("act", lambda e: e.activation(out=tmp["var"][:, 0:T], in_=tmp["var"][:, 0:T], func=AF.Sqrt), reads=[bs], writes=[bs])

        def g3():
            P.op("dve", lambda e: e.reciprocal(out=tmp["rstd"][:, 0:T], in_=tmp["var"][:, 0:T]), reads=[bs], writes=[bs])
            P.op("dve", lambda e: e.scalar_tensor_tensor(out=tmp["nmr"][:, 0:T], in0=tmp["mean"][:, 0:T], scalar=-1.0, in1=tmp["rstd"][:, 0:T],
                                                         op0=ALU.mult, op1=ALU.mult),
                 reads=[bs], writes=[bs])
        groups += [g0, g1, g2, g3]

        def mk(i):
            def g():
                if i < NCH:
                    k = i
                    P.op("dve", lambda e: e.tensor_tensor(out=z(k), in0=z(k), in1=tmp["rstd"][:, 0:T], op=ALU.mult),
                         reads=[bs, b_z(k)], writes=[b_z(k)])
                if 0 <= i - 1 < NCH:
                    k = i - 1
                    P.op(add_eng, lambda e: e.tensor_tensor(out=z(k), in0=z(k), in1=tmp["nmr"][:, 0:T], op=ALU.add),
                         reads=[bs, b_z(k)], writes=[b_z(k)])
                if 0 <= i - 2 < NCH:
                    k = i - 2
                    P.op("act", lambda e: e.activation(out=z(k), in_=z(k), func=AF.Identity,
                                                       scale=self.lnv(ln_idx, 0, k), bias=self.lnv(ln_idx, 1, k)),
                         reads=[b_z(k), self.b_const], writes=[b_z(k)])
            return g
        groups += [mk(i) for i in range(NCH + 2)]
        return groups

    def make_precast(self):
        P = self.P

        def wo(i):
            src = self.na_wo if i == 0 else self.fn_wo
            for k in range(NCH):
                self.bg_pre.append(lambda k=k: P.dma("pool", lambda e: e.dma_start(out=self.wo_bf[i][:, k, :], in_=src[:, k, :]),
                                                     writes=[self.b_wobf[i]], sem_buf=self.b_wobf[i]))

        def ff(l):
            for k in range(NCH):
                for c0 in range(0, 2 * DFF, 1408):
                    self.bg_pre.append(lambda k=k, c0=c0: P.dma("pool", lambda e: e.dma_start(out=self.wup_bf[l][:, k, c0:c0 + 1408],
                                                                                              in_=self.w_up[l, :, k, c0:c0 + 1408]),
                                                                writes=[self.b_wupbf[l]], sem_buf=self.b_wupbf[l]))
            for f in range(0, NF, 2):
                self.bg_pre.append(lambda f=f: P.dma("pool", lambda e: e.dma_start(out=self.wdn_bf[l][:, f:f + 2, :], in_=self.w_dn[l, :, f:f + 2, :]),
                                                     writes=[self.b_wdnbf[l]], sem_buf=self.b_wdnbf[l]))
        wo(0)
        ff(0)
        wo(1)
        ff(1)

    def drain_bg(self, n_ada=None, n_pre=None):
        for q, n in ((self.bg_ada, n_ada), (self.bg_pre, n_pre)):
            n = len(q) if n is None else min(n, len(q))
            for _ in range(n):
                q.pop(0)()

    def drain(self, n=None):
        q = self.deferred
        n = len(q) if n is None else min(n, len(q))
        for _ in range(n):
            q.pop(0)()

    def alloc_ln_tmp(self, ph, T, tag):
        P = self.P
        tmp = {"zb": self.sb(ph, tag + "zb", [128, 4, T], BF16), "zsq": self.sb(ph, tag + "zsq", [128, 4, T], BF16)}
        for n in ("mean", "msq"):
            tmp[n] = self.sb(ph, tag + n, [128, T], F32)
        tmp["var"] = tmp["msq"]
        tmp["rstd"] = tmp["msq"]
        tmp["nmr"] = tmp["mean"]

        b_tmp = {"zb": P.bufs(4, "zb"), "zsq": P.bufs(4, "zsq"), "st": P.buf("st")}
        return tmp, b_tmp

    def proj_ln(self, ph, inT, b_in, w_dram, l, src, b_src, dst, b_dst, ln_idx):
        P = self.P
        w_sb = self.sb(ph, "pl_w", [128, NCH, D], BF16)
        b_w = P.buf("pl_w")
        self.drain_bg()
        wi = 0 if l == 0 else 1
        for hh in range(2):
            P.dma("sp", lambda e: e.dma_start(out=w_sb[:, hh * 4:(hh + 1) * 4, :], in_=self.wo_bf[wi][:, hh * 4:(hh + 1) * 4, :]),
                  reads=[self.b_wobf[wi]], writes=[b_w], sem_buf=b_w)
        NS = 3
        xr = [self.sb(ph, "pl_x%d" % i, [128, NCH, 512], F32) for i in range(NS)]
        b_xr = [P.bufs(NCH, "pl_x%d_" % i) for i in range(NS)]
        tmp, b_tmp = self.alloc_ln_tmp(ph, 512, "pl_")
        cnt = {"y": 0, "z": 0}

        sx = P.bufs(NS, "pl_sx")

        def load(tb):
            slot = tb % NS
            P.dma_group("sp", [(lambda e, hh=hh: e.dma_start(out=xr[slot][:, hh * 4:(hh + 1) * 4, :],
                                                             in_=src[:, hh * 4:(hh + 1) * 4, tb * 512:(tb + 1) * 512])) for hh in range(2)],
                        reads=[b_src[tb]], writes=b_xr[slot], sem_buf=sx[slot])

        def sbank(tb):
            sb0 = 3 + 2 * (tb % 2)
            return (sb0, 0), (sb0 + 1, 0)

        def A(tb):
            slot = tb % NS
            tok = slice(tb * 512, (tb + 1) * 512)
            s1, s2 = sbank(tb)
            prev = None
            for k in range(NCH):
                yb = cnt["y"] % 3
                cnt["y"] += 1
                for kk in range(NCH):
                    P.op("pe", lambda e: e.matmul(self.bank(yb), lhsT=w_sb[:, kk, k * 128:(k + 1) * 128], rhs=inT[:, kk, tok],
                                                  start=(kk == 0), stop=(kk == NCH - 1)),
                         reads=[b_w] + (b_in[tb] if isinstance(b_in[tb], list) else [b_in[tb]]), writes=[self.pb[yb]])
                P.op("dve", lambda e: e.scalar_tensor_tensor(out=xr[slot][:, k, :], in0=self.bank(yb), scalar=self.mod(l, "g1", k),
                                                             in1=xr[slot][:, k, :], op0=ALU.mult, op1=ALU.add),
                     reads=[self.bmod(l, "g1"), b_xr[slot][k]], writes=[self.pb[yb], b_xr[slot][k]])
                r = cnt["z"] % 4
                cnt["z"] += 1
                self.ln_pre(512, xr[slot][:, k, :], b_xr[slot][k], r, tmp, b_tmp)
                if prev is not None:
                    self.ln_stats_mm(512, prev[0], prev[1], tmp, b_tmp, s1, s2, False)
                prev = (k, r)
                self.drain(2)
            self.ln_stats_mm(512, prev[0], prev[1], tmp, b_tmp, s1, s2, False)

        def B(tb):
            slot = tb % NS
            s1, s2 = sbank(tb)
            grp = self.ln_finish_groups(512, lambda k: xr[slot][:, k, :], lambda k: b_xr[slot][k], ln_idx, tmp, b_tmp, s1, s2,
                                        add_eng="pool")
            for g in grp[:4]:
                g()
            self.deferred += grp[4:]

            def out():
                P.dma_group("sp", [(lambda e, hh=hh: e.dma_start(out=dst[:, hh * 4:(hh + 1) * 4, tb * 512:(tb + 1) * 512],
                                                                 in_=xr[slot][:, hh * 4:(hh + 1) * 4, :])) for hh in range(2)],
                            reads=b_xr[slot], writes=[b_dst[tb]], sem_buf=sx[slot])
            self.deferred.append(out)

        load(0)
        load(1)
        A(0)
        B(0)
        for tb in range(1, 8):
            A(tb)
            self.drain()
            if tb + 1 < 8:
                load(tb + 1)
            B(tb)
        self.drain()
    def phase_attn(self, src, b_src, dst, b_dst):
        P, nc = self.P, self.nc
        l = 0
        with ExitStack() as ph:
            attnT = self.sb(ph, "attnT", [128, NCH, S], BF16)
            b_at = P.bufs(8, "attnT")
            with ExitStack() as ph2:
                uT = self.sb(ph2, "uT", [128, NCH, S], BF16)
                b_u = P.bufs(8, "uT")
                with ExitStack() as ph3:
                    self.stream_u(ph3, src, b_src, l, "sc1", "sh1", uT, b_u)
                    P.barrier()
                wq = [self.sb(ph2, "wq%d" % i, [128, NCH, 3, 128], BF16) for i in range(2)]
                tabs = [self.sb(ph2, "tab%d" % i, [128, 2, NSLOT * 64], BF16) for i in range(2)]
                b_wq = P.bufs(2, "wq")
                b_tab = P.bufs(2, "tab")
                qT = self.sb(ph2, "qT", [128, S], BF16)
                kz = [self.sb(ph2, "kz%d" % i, [128, S], BF16) for i in range(2)]
                vt = self.sb(ph2, "vt", [128, 32, 128], BF16)
                PT = [self.sb(ph2, "PT%d" % i, [128, 768], BF16) for i in range(2)]
                rden = [self.sb(ph2, "rden%d" % i, [128, 256], F32) for i in range(2)]
                b_q, b_k, b_v = P.bufs(8, "q"), P.bufs(8, "k"), P.bufs(8, "v")
                b_PT = P.bufs(2, "PT")
                b_rd = P.bufs(2, "rden")
                b_kz0 = P.buf("kz0")
                P.op("pool", lambda e: e.memset(kz[0][64:128, :], 0.0), writes=[b_kz0])
                P.op("pool", lambda e: e.memset(kz[1][0:64, :], 0.0), writes=[b_kz0])
                evac_i = 0
                for i in range(NCH):
                    ws = i % 2
                    P.dma("pool", (lambda i, ws: lambda e: e.dma_start(out=wq[ws].rearrange("p k t e -> p (k t e)")[:, 0:1536],
                                                                       in_=self.wqkv[i, :, 0:1536]))(i, ws), writes=[b_wq[ws]], sem_buf=b_wq[ws])
                    P.dma("pool", (lambda i, ws: lambda e: e.dma_start(out=wq[ws].rearrange("p k t e -> p (k t e)")[:, 1536:3072],
                                                                       in_=self.wqkv[i, :, 1536:3072]))(i, ws), writes=[b_wq[ws]], sem_buf=b_wq[ws])
                    for hp in range(2):
                        P.dma("pool", (lambda i, ws, hp: lambda e: e.dma_start(out=tabs[ws][:, hp, :], in_=self.tab[2 * i + hp, :, :]))(i, ws, hp),
                              writes=[b_tab[ws]], sem_buf=b_tab[ws])
                    for tb in range(8):
                        tok = slice(tb * 512, (tb + 1) * 512)
                        for t in range(2):
                            bk = evac_i % 4
                            evac_i += 1
                            for kk in range(NCH):
                                P.op("pe", (lambda kk, t, bk, tok=tok: lambda e: e.matmul(self.bank(bk), lhsT=wq[ws][:, kk, t, :], rhs=uT[:, kk, tok],
                                                                                          start=(kk == 0), stop=(kk == NCH - 1)))(kk, t, bk),
                                     reads=[b_wq[ws], b_u[tb]], writes=[self.pb[bk]])
                            if t == 0:
                                P.op("act", (lambda bk, tok=tok: lambda e: e.activation(out=qT[:, tok], in_=self.bank(bk), func=AF.Copy, scale=0.125))(bk),
                                     writes=[self.pb[bk], b_q[tb]])
                            else:
                                P.op("dve", (lambda bk, tok=tok: lambda e: e.tensor_copy(out=kz[0][0:64, tok], in_=self.bank(bk, 0, 512, 0, 64)))(bk),
                                     reads=[b_kz0], writes=[self.pb[bk], b_k[tb]])
                                P.op("act", (lambda bk, tok=tok: lambda e: e.activation(out=kz[1][64:128, tok], in_=self.bank(bk, 0, 512, 64, 128),
                                                                                         func=AF.Copy))(bk),
                                     reads=[b_kz0], writes=[self.pb[bk], b_k[tb]])
                        bk = evac_i % 4
                        evac_i += 1
                        for jj in range(4):
                            j = tb * 4 + jj
                            for kk in range(NCH):
                                P.op("pe", (lambda kk, jj, j, bk: lambda e: e.matmul(self.bank(bk, jj * 128, (jj + 1) * 128),
                                                                                     lhsT=uT[:, kk, j * 128:(j + 1) * 128], rhs=wq[ws][:, kk, 2, :],
                                                                                     start=(kk == 0), stop=(kk == NCH - 1)))(kk, jj, j, bk),
                                     reads=[b_wq[ws], b_u[tb]], writes=[self.pb[bk]])
                        P.op("dve", (lambda tb, bk: lambda e: e.tensor_copy(out=vt[:, tb * 4:(tb + 1) * 4, :].rearrange("p a b -> p (a b)"),
                                                                             in_=self.bank(bk)))(tb, bk),
                             writes=[self.pb[bk], b_v[tb]])
                    def qk(hp, j):
                        sbi = j % 2
                        bA, bB = 2 * sbi, 2 * sbi + 1
                        qlo, qhi = na_row_range(j)
                        n = qhi - qlo + 1
                        nA = min(n, 8)
                        nB = n - nA
                        kt = j // 4
                        for (bk, r0, nr) in ((bA, qlo, nA), (bB, qlo + 8, nB)):
                            if nr <= 0:
                                continue
                            qtbs = sorted(set([(r0 * 64) // 512, ((r0 + nr) * 64 - 1) // 512]))
                            P.op("pe", lambda e: e.matmul(self.bank(bk, 0, nr * 64), lhsT=kz[hp][:, j * 128:(j + 1) * 128],
                                                          rhs=qT[:, r0 * 64:(r0 + nr) * 64], start=True, stop=False, skip_group_check=True),
                                 reads=[b_k[kt]] + [b_q[x] for x in qtbs], writes=[self.pb[bk]])
                            runs = []
                            for r in range(r0, r0 + nr):
                                sl = na_slot(j, r)
                                if runs and runs[-1][1] + runs[-1][2] == sl and sl < 14:
                                    runs[-1][2] += 1
                                else:
                                    runs.append([r, sl, 1])
                            for ri, (r, sl, cnt) in enumerate(runs):
                                P.op("pe", lambda e: e.matmul(self.bank(bk, (r - r0) * 64, (r - r0 + cnt) * 64), lhsT=self.ident_bf[:],
                                                              rhs=tabs[ws][:, hp, sl * 64:(sl + cnt) * 64], start=False,
                                                              stop=(ri == len(runs) - 1), skip_group_check=True),
                                     reads=[b_tab[ws], self.b_const], writes=[self.pb[bk]])
                            P.op("act", lambda e: e.activation(out=PT[sbi][:, (r0 - qlo) * 64:(r0 - qlo + nr) * 64],
                                                               in_=self.bank(bk, 0, nr * 64), func=AF.Exp),
                                 writes=[self.pb[bk], b_PT[sbi]])

                    def pv(hp, j):
                        sbi = j % 2
                        qlo, qhi = na_row_range(j)
                        kt = j // 4
                        for b in range(qlo // 4, qhi // 4 + 1):
                            r0 = max(qlo, 4 * b)
                            r1 = min(qhi, 4 * b + 3)
                            ob = 4 + (b % 4)
                            first = (j == max(0, 2 * b - 2))
                            lastj = (j == min(31, 2 * b + 3))
                            pc0, pc1 = (r0 - qlo) * 64, (r1 - qlo + 1) * 64
                            oc0, oc1 = (r0 - 4 * b) * 64, (r1 - 4 * b + 1) * 64
                            P.op("pe", lambda e: e.matmul(self.bank(ob, oc0, oc1), lhsT=vt[:, j, :], rhs=PT[sbi][:, pc0:pc1], start=first, stop=False,
                                                          skip_group_check=True),
                                 reads=[b_v[kt], b_PT[sbi]], writes=[self.pb[ob]])
                            P.op("pe", lambda e: e.matmul(self.bank(ob, 256 + oc0, 256 + oc1), lhsT=self.ones_bf[:], rhs=PT[sbi][:, pc0:pc1],
                                                          start=False, stop=lastj, skip_group_check=True),
                                 reads=[self.b_const, b_PT[sbi]], writes=[self.pb[ob]])
                            if lastj:
                                rs_ = b % 2
                                p0, p1 = hp * 64, hp * 64 + 64
                                P.op("dve", lambda e: e.reciprocal(out=rden[rs_][p0:p1, :], in_=self.bank(ob, 256, 512, p0, p1)),
                                     writes=[self.pb[ob], b_rd[rs_]])
                                P.op("dve", lambda e: e.tensor_tensor(out=attnT[p0:p1, i, b * 256:(b + 1) * 256], in0=self.bank(ob, 0, 256, p0, p1),
                                                                      in1=rden[rs_][p0:p1, :], op=ALU.mult),
                                     reads=[b_rd[rs_]], writes=[self.pb[ob], b_at[b // 2]])

                    steps = [(hp, j) for hp in range(2) for j in range(32)]
                    qk(*steps[0])
                    for si, (hp, j) in enumerate(steps):
                        step = i * 64 + si
                        if step % 4 == 3:
                            self.drain_bg(1, 0)
                        if step % 4 == 1:
                            self.drain_bg(0, 1)
                        if si + 1 < len(steps):
                            qk(*steps[si + 1])
                        pv(hp, j)
                self.drain_bg()
                P.barrier()
            self.proj_ln(ph, attnT, b_at, self.na_wo, l, src, b_src, dst, b_dst, 0)
            P.barrier()

    def phase_four(self, src, b_src, dst, b_dst):
        P, nc = self.P, self.nc
        l = 1
        with ExitStack() as ph:
            uT = self.sb(ph, "uyT", [128, NCH, S], BF16)
            yT = uT
            b_u = P.bufs(8, "uTc")
            with ExitStack() as ph2:
                xin = [self.sb(ph2, "fx%d" % i, [128, S], F32) for i in range(2)]
                b_x = P.bufs(2, "fx")

                def stream_chunk(k):
                    slot = k % 2
                    P.dma("sp", lambda e: e.dma_start(out=xin[slot][:], in_=src[:, k, :]), reads=list(b_src), writes=[b_x[slot]], sem_buf=b_x[slot])
                    for pc in range(4):
                        cs_ = slice(pc * 1024, (pc + 1) * 1024)
                        if pc % 2 == 0:
                            P.op("act", lambda e: e.activation(out=uT[:, k, cs_], in_=xin[slot][:, cs_], func=AF.Identity,
                                                               scale=self.mod(l, "sc1", k), bias=self.mod(l, "sh1", k)),
                                 reads=[b_x[slot], self.bmod(l, "sc1"), self.bmod(l, "sh1")], writes=[b_u[k]])
                        else:
                            P.op("dve", lambda e: e.tensor_scalar(out=uT[:, k, cs_], in0=xin[slot][:, cs_],
                                                                  scalar1=self.mod(l, "sc1", k), scalar2=self.mod(l, "sh1", k),
                                                                  op0=ALU.mult, op1=ALU.add),
                                 reads=[b_x[slot], self.bmod(l, "sc1"), self.bmod(l, "sh1")], writes=[b_u[k]])
                tC = self.sb(ph2, "tC", [128, 256], BF16)
                tR = self.sb(ph2, "tR", [128, 128], BF16)
                tV = self.sb(ph2, "tV", [128, 64, 64], BF16)
                b_t = P.buf("ftab")
                P.dma("sp", lambda e: e.dma_start(out=tC[:], in_=self.tabC[:, :]), writes=[b_t], sem_buf=b_t)
                P.dma("sp", lambda e: e.dma_start(out=tR[:], in_=self.tabR[:, :]), writes=[b_t], sem_buf=b_t)
                P.dma("sp", lambda e: e.dma_start(out=tV.rearrange("p a b -> p (a b)"), in_=self.tabV[:, :]), writes=[b_t], sem_buf=b_t)
                XCs = [self.sb(ph2, "XC%d" % i, [128, 64, 128], BF16) for i in range(2)]
                As = [self.sb(ph2, "A%d" % i, [128, 128, 64], BF16) for i in range(2)]
                b_XCs = [P.bufs(16, "XC%d_" % i) for i in range(2)]
                b_As = [P.bufs(16, "A%d_" % i) for i in range(2)]
                ev = {"n": 0}

                def nextbank():
                    bk = ev["n"] % 8
                    ev["n"] += 1
                    return bk

                def evac(out_ap, in_ap, writes):
                    if ev["n"] % 2 == 0:
                        P.op("act", lambda e: e.activation(out=out_ap, in_=in_ap, func=AF.Copy), writes=writes)
                    else:
                        P.op("dve", lambda e: e.tensor_copy(out=out_ap, in_=in_ap), writes=writes)

                def stageC(g):
                    XC, b_XC = XCs[g % 2], b_XCs[g % 2]
                    ug = uT[:, g, :].rearrange("p (r c) -> p c r", c=64)
                    for cq in range(16):
                        bk = nextbank()
                        for cc in range(4):
                            c = cq * 4 + cc
                            P.op("pe", lambda e: e.matmul(self.bank(bk, cc * 128, (cc + 1) * 128, 0, 64), lhsT=ug[:, c, :], rhs=tC[:, 0:128],
                                                          start=True, stop=True, skip_group_check=True),
                                 reads=[b_u[g], b_t], writes=[self.pb[bk]])
                            P.op("pe", lambda e: e.matmul(self.bank(bk, cc * 128, (cc + 1) * 128, 64, 128), lhsT=ug[:, c, :], rhs=tC[:, 128:256],
                                                          start=True, stop=True, skip_group_check=True),
                                 reads=[b_u[g], b_t], writes=[self.pb[bk]])
                        evac(XC[:, cq * 4:(cq + 1) * 4, :].rearrange("p a b -> p (a b)"), self.bank(bk), [self.pb[bk], b_XC[cq]])

                def stageR(g):
                    XC, b_XC = XCs[g % 2], b_XCs[g % 2]
                    A, b_A = As[g % 2], b_As[g % 2]
                    for mq in range(16):
                        bk = nextbank()
                        for mm in range(8):
                            m = mq * 8 + mm
                            P.op("pe", lambda e: e.matmul(self.bank(bk, mm * 64, (mm + 1) * 64, 0, 64), lhsT=XC[:, :, m], rhs=tR[:, 0:64],
                                                          start=True, stop=True, skip_group_check=True),
                                 reads=b_XC + [b_t], writes=[self.pb[bk]])
                            P.op("pe", lambda e: e.matmul(self.bank(bk, mm * 64, (mm + 1) * 64, 64, 128), lhsT=XC[:, :, m], rhs=tR[:, 64:128],
                                                          start=True, stop=True, skip_group_check=True),
                                 reads=b_XC + [b_t], writes=[self.pb[bk]])
                        evac(A[:, mq * 8:(mq + 1) * 8, :].rearrange("p a b -> p (a b)"), self.bank(bk), [self.pb[bk], b_A[mq]])

                def stageV(g):
                    A, b_A = As[g % 2], b_As[g % 2]
                    for kq in range(8):
                        bk = nextbank()
                        for kk in range(8):
                            ka = kq * 8 + kk
                            P.op("pe", lambda e: e.matmul(self.bank(bk, kk * 64, (kk + 1) * 64), lhsT=A[:, :, ka], rhs=tV[:, ka, :],
                                                          start=True, stop=True, skip_group_check=True),
                                 reads=b_A + [b_t], writes=[self.pb[bk]])
                        outv = yT[:, g, :].rearrange("p (kb ka) -> p kb ka", ka=64)[:, :, kq * 8:(kq + 1) * 8]
                        inv = self.bank(bk).rearrange("p (kk kb) -> p kb kk", kb=64)
                        evac(outv, inv, [self.pb[bk], b_u[g]])

                stream_chunk(0)
                stream_chunk(1)
                stageC(0)
                stream_chunk(2)
                stageC(1)
                stageR(0)
                for g in range(NCH):
                    if g + 3 < NCH:
                        stream_chunk(g + 3)
                    if g + 2 < NCH:
                        stageC(g + 2)
                    if g + 1 < NCH:
                        stageR(g + 1)
                    stageV(g)
                P.barrier()
            self.proj_ln(ph, yT, [list(b_u)] * 8, self.fn_wo, l, src, b_src, dst, b_dst, 2)
            P.barrier()

    def phase_ffn(self, l, src, b_src, dst, b_dst):
        P, nc = self.P, self.nc
        T = 256
        NT = S // T
        ln_idx = 2 * l + 1
        with ExitStack() as ph:
            wup = self.sb(ph, "wup", [128, NCH, 2 * DFF], BF16)
            wdn = self.sb(ph, "wdn", [128, NF, D], BF16)
            self.drain_bg()
            WG = [(0, 4), (4, 12), (12, 22)]
            DG = [(0, 8), (8, 22)]
            b_wupg, b_wdng = P.bufs(len(WG), "wup"), P.bufs(len(DG), "wdn")
            wg_of = lambda fc: [i for i, (a, b) in enumerate(WG) if a <= fc < b][0]
            dg_of = lambda fc: [i for i, (a, b) in enumerate(DG) if a <= fc < b][0]
            for g, (f0, f1) in enumerate(WG):
                fns = []
                for base in (0, DFF):
                    for q0 in range(f0, f1, 4):
                        q1 = min(q0 + 4, f1)
                        fns.append(lambda e, base=base, q0=q0, q1=q1: e.dma_start(out=wup[:, :, base + q0 * 128:base + q1 * 128],
                                                                                  in_=self.wup_bf[l][:, :, base + q0 * 128:base + q1 * 128]))
                P.dma_group("sp", fns, reads=[self.b_wupbf[l]], writes=[b_wupg[g]], sem_buf=b_wupg[g])
            for g, (f0, f1) in enumerate(DG):
                fns = [(lambda e, q0=q0: e.dma_start(out=wdn[:, q0:min(q0 + 4, f1), :], in_=self.wdn_bf[l][:, q0:min(q0 + 4, f1), :]))
                       for q0 in range(f0, f1, 4)]
                P.dma_group("sp", fns, reads=[self.b_wdnbf[l]], writes=[b_wdng[g]], sem_buf=b_wdng[g])
            NS = 3
            xh = [self.sb(ph, "ff_x%d" % i, [128, NCH, T + 2], F32) for i in range(NS)]
            u2 = [self.sb(ph, "ff_u%d" % i, [128, NCH, T + 2], BF16) for i in range(2)]
            hh_ = [self.sb(ph, "ff_h%d" % i, [128, NF, T], BF16) for i in range(2)]
            a1 = [self.sb(ph, "ff_a%d" % i, [128, T], F32) for i in range(3)]
            ge = [self.sb(ph, "ff_g%d" % i, [128, T], BF16) for i in range(3)]
            gs = [self.sb(ph, "ff_gs%d" % i, [128, T], BF16) for i in range(3)]
            b_xh = [P.bufs(NCH, "ffx%d_" % i) for i in range(NS)]
            sxh = P.bufs(NS, "ff_sx")
            b_u2, b_h = P.bufs(2, "ffu"), P.bufs(2, "ffh")
            b_a1, b_ge, b_gs = P.bufs(3, "ffa"), P.bufs(3, "ffg"), P.bufs(3, "ffgs")
            tmp, b_tmp = self.alloc_ln_tmp(ph, T, "ff_")
            cw = lambda a, fc: self.convp_sb[:, l, a * NF + fc: a * NF + fc + 1]
            cnt = {"z": 0}

            def load_u(t):
                slot = t % NS
                us = t % 2
                s0 = t * T
                lo = max(s0 - 1, 0)
                hi = min(s0 + T + 1, S)
                c_lo = lo - (s0 - 1)
                w = hi - lo
                tb_set = sorted(set([lo // (S // len(b_src)), (hi - 1) // (S // len(b_src))]))
                P.dma_group("sp", [(lambda e, hh=hh: e.dma_start(out=xh[slot][:, hh * 4:(hh + 1) * 4, c_lo:c_lo + w],
                                                                 in_=src[:, hh * 4:(hh + 1) * 4, lo:hi])) for hh in range(2)],
                            reads=[b_src[x] for x in tb_set], writes=b_xh[slot], sem_buf=sxh[slot])
                for k in range(NCH):
                    P.op("pool", lambda e: e.tensor_scalar(out=u2[us][:, k, c_lo:c_lo + w], in0=xh[slot][:, k, c_lo:c_lo + w],
                                                           scalar1=self.mod(l, "sc2", k), scalar2=self.mod(l, "sh2", k),
                                                           op0=ALU.mult, op1=ALU.add),
                         reads=[b_xh[slot][k], self.bmod(l, "sc2"), self.bmod(l, "sh2")], writes=[b_u2[us]])
                if t == 0:
                    P.op("pool", lambda e: e.memset(u2[us][:, :, 0:1], 0.0), writes=[b_u2[us]])
                if t == NT - 1:
                    P.op("pool", lambda e: e.memset(u2[us][:, :, T + 1:T + 2], 0.0), writes=[b_u2[us]])

            def up(t):
                slot = t % NS
                us = t % 2

                def tail(fc):
                    r = fc % 3
                    P.op("act", lambda e: e.activation(out=ge[r][:], in_=a1[r][:], func=AF.Gelu_apprx_tanh), reads=[b_a1[r]], writes=[b_ge[r]])
                    P.op("pool", lambda e: e.tensor_tensor(out=hh_[us][:, fc, :], in0=ge[r][:], in1=gs[r][:], op=ALU.mult),
                         reads=[b_ge[r], b_gs[r]], writes=[b_h[us]])
                for fc in range(NF):
                    ab = fc % 3
                    gb = 3 + fc % 2
                    r = fc % 3
                    for k in range(NCH):
                        P.op("pe", lambda e: e.matmul(self.bank(ab, 0, T + 2), lhsT=wup[:, k, fc * 128:(fc + 1) * 128], rhs=u2[us][:, k, :],
                                                      start=(k == 0), stop=(k == NCH - 1)),
                             reads=[b_wupg[wg_of(fc)], b_u2[us]], writes=[self.pb[ab]])
                    for k in range(NCH):
                        P.op("pe", lambda e: e.matmul(self.bank(gb, 0, T), lhsT=wup[:, k, DFF + fc * 128:DFF + (fc + 1) * 128],
                                                      rhs=u2[us][:, k, 1:T + 1], start=(k == 0), stop=(k == NCH - 1)),
                             reads=[b_wupg[wg_of(fc)], b_u2[us]], writes=[self.pb[gb]])
                    P.op("act", lambda e: e.activation(out=a1[r][:], in_=self.bank(ab, 1, T + 1), func=AF.Identity,
                                                       scale=cw(1, fc), bias=cw(3, fc)),
                         reads=[self.b_const], writes=[self.pb[ab], b_a1[r]])
                    P.op("act", lambda e: e.activation(out=gs[r][:], in_=self.bank(gb, 0, T), func=AF.Copy),
                         writes=[self.pb[gb], b_gs[r]])
                    P.op("dve", lambda e: e.scalar_tensor_tensor(out=a1[r][:], in0=self.bank(ab, 0, T), scalar=cw(0, fc), in1=a1[r][:],
                                                                 op0=ALU.mult, op1=ALU.add),
                         reads=[self.b_const, b_a1[r]], writes=[self.pb[ab], b_a1[r]])
                    P.op("dve", lambda e: e.scalar_tensor_tensor(out=a1[r][:], in0=self.bank(ab, 2, T + 2), scalar=cw(2, fc), in1=a1[r][:],
                                                                 op0=ALU.mult, op1=ALU.add),
                         reads=[self.b_const, b_a1[r]], writes=[self.pb[ab], b_a1[r]])
                    if fc >= 1:
                        tail(fc - 1)
                    if fc in (0, 2, 4, 7) or fc >= 9:
                        self.drain(1)
                tail(NF - 1)

            def down_and_ln(t):
                slot = t % NS
                us = t % 2
                s0 = t * T
                s1, s2 = (7, 0), (7, T)
                pend = []
                for p in range(4):
                    yb = 5 + p % 2
                    ks = (2 * p, 2 * p + 1)
                    for k in ks:
                        c0 = (k % 2) * T
                        for fc in range(NF):
                            P.op("pe", lambda e: e.matmul(self.bank(yb, c0, c0 + T), lhsT=wdn[:, fc, k * 128:(k + 1) * 128], rhs=hh_[us][:, fc, :],
                                                          start=(fc == 0), stop=(fc == NF - 1), skip_group_check=True),
                                 reads=[b_wdng[dg_of(fc)], b_h[us]], writes=[self.pb[yb]])
                    newp = []
                    for k in ks:
                        c0 = (k % 2) * T
                        P.op("dve", lambda e: e.scalar_tensor_tensor(out=xh[slot][:, k, 1:T + 1], in0=self.bank(yb, c0, c0 + T),
                                                                     scalar=self.mod(l, "g2", k), in1=xh[slot][:, k, 1:T + 1],
                                                                     op0=ALU.mult, op1=ALU.add),
                             reads=[self.bmod(l, "g2"), b_xh[slot][k]], writes=[self.pb[yb], b_xh[slot][k]])
                        r = cnt["z"] % 4
                        cnt["z"] += 1
                        self.ln_pre(T, xh[slot][:, k, 1:T + 1], b_xh[slot][k], r, tmp, b_tmp)
                        newp.append((k, r))
                    for (k, r) in pend:
                        self.ln_stats_mm(T, k, r, tmp, b_tmp, s1, s2, True)
                    pend = newp
                for (k, r) in pend:
                    self.ln_stats_mm(T, k, r, tmp, b_tmp, s1, s2, True)
                self.deferred += self.ln_finish_groups(T, lambda k: xh[slot][:, k, 1:T + 1], lambda k: b_xh[slot][k], ln_idx, tmp, b_tmp, s1, s2)
                tbd = s0 // (S // len(b_dst))

                def out():
                    P.dma_group("sp", [(lambda e, hh=hh: e.dma_start(out=dst[:, hh * 4:(hh + 1) * 4, s0:s0 + T],
                                                                     in_=xh[slot][:, hh * 4:(hh + 1) * 4, 1:T + 1])) for hh in range(2)],
                                reads=b_xh[slot], writes=[b_dst[tbd]], sem_buf=sxh[slot])
                self.deferred.append(out)

            load_u(0)
            up(0)
            load_u(1)
            for t in range(NT):
                if t + 1 < NT:
                    up(t + 1)
                self.drain()
                if t + 2 < NT:
                    load_u(t + 2)
                down_and_ln(t)
            self.drain()
            P.barrier()
def prep_inputs(x, c, ada_w, ada_b, na_w_qkv, na_rpb, na_w_o, fn_w_o, ln1_g, ln1_b,
                ffn_w_up, ffn_conv_w, ffn_conv_b, ffn_w_down, ln2_g, ln2_b):
    f = lambda a: np.ascontiguousarray(np.asarray(a, dtype=np.float32))
    x, c = f(x), f(c)
    B = x.shape[0]
    tabC, tabR, tabV = make_dft_tables()
    chunk = lambda w: f(w.reshape(NCH, 128, -1).transpose(1, 0, 2))
    common = {
        "ada_w": f(np.stack([chunk(f(ada_w)[l]) for l in range(2)])),
        "ada_b": f(ada_b),
        "wqkv": f(f(na_w_qkv)[0].reshape(NCH, 128, 3, NCH, 128).transpose(3, 1, 0, 2, 4).reshape(NCH, 128, NCH * 3 * 128)),
        "tab": f(make_bias_table(f(na_rpb)[0])),
        "na_wo": chunk(f(na_w_o)[0]),
        "fn_wo": chunk(f(fn_w_o)[0]),
        "lnp": f(np.stack([np.stack([f(g)[l].reshape(NCH, 128).T, f(b)[l].reshape(NCH, 128).T], axis=1)
                           for l in range(2) for (g, b) in ((ln1_g, ln1_b), (ln2_g, ln2_b))], axis=1)),
        "w_up": f(np.stack([chunk(f(ffn_w_up)[l]) for l in range(2)])),
        "w_dn": f(np.stack([f(ffn_w_down)[l].reshape(NF, 128, D).transpose(1, 0, 2) for l in range(2)])),
        "convp": f(np.stack([np.concatenate([f(ffn_conv_w)[l], f(ffn_conv_b)[l][None]], axis=0).reshape(4, NF, 128).transpose(2, 0, 1)
                             for l in range(2)])),
        "tabC": tabC, "tabR": tabR, "tabV": tabV,
        "ident": _bf(np.eye(128)),
    }
    in_maps = []
    for b in range(B):
        m = dict(common)
        m["xT"] = f(x[b].T.reshape(NCH, 128, S).transpose(1, 0, 2))
        m["ccol"] = f(c[b].reshape(NCH, 128).T)
        in_maps.append(m)
    return in_maps


_NC_CACHE = {}


def get_nc(phases=("ada", "attn", "ffn0", "four", "ffn1")):
    key = tuple(phases)
    if key not in _NC_CACHE:
        _NC_CACHE[key] = Builder(phases).build()
    return _NC_CACHE[key]


def kernel(x, c, ada_w, ada_b, na_w_qkv, na_rpb, na_w_o, fn_w_o, ln1_g, ln1_b,
           ffn_w_up, ffn_conv_w, ffn_conv_b, ffn_w_down, ln2_g, ln2_b):
    in_maps = prep_inputs(x, c, ada_w, ada_b, na_w_qkv, na_rpb, na_w_o, fn_w_o, ln1_g, ln1_b,
                          ffn_w_up, ffn_conv_w, ffn_conv_b, ffn_w_down, ln2_g, ln2_b)
    nc = get_nc()
    res = run_bass_kernel_spmd(nc, in_maps, core_ids=list(range(len(in_maps))))
    outs = [np.asarray(r["outT"]).transpose(1, 0, 2).reshape(D, S).T for r in res.results]
    return np.ascontiguousarray(np.stack(outs, axis=0).astype(np.float32))
```

```python
import math
from contextlib import ExitStack

import numpy as np
import ml_dtypes

import concourse.bass as bass
import concourse.mybir as mybir
from concourse.bass_utils import run_bass_kernel_spmd

F32 = mybir.dt.float32
BF16 = mybir.dt.bfloat16
AF = mybir.ActivationFunctionType
ALU = mybir.AluOpType

D = 1024
S = 4096
NCH = 8
DFF = 2816
NF = 22
NH = 16
ALPHA = math.sqrt(2.0)
EPS_P = 1e-5 / (ALPHA * ALPHA)
NEG = -30000.0
NSLOT = 16

ENGS = ("pe", "act", "dve", "pool", "sp")


class Buf:
    __slots__ = ("name", "last_w", "readers", "sem", "dcount")

    def __init__(self, name):
        self.name = name
        self.last_w = None
        self.readers = []
        self.sem = None
        self.dcount = 0


class _Rec:
    def __init__(self):
        self.call = None

    def __getattr__(self, name):
        def f(*a, **kw):
            assert self.call is None
            self.call = (name, a, kw)
            return self
        return f


def _record(fn):
    r = _Rec()
    fn(r)
    assert r.call is not None
    return r.call


class Prog:
    def __init__(self, nc, ctx):
        self.nc = nc
        self.ctx = ctx
        self.q = {e: [] for e in ENGS}
        self.seq = {e: 0 for e in ENGS}
        self.esem = {e: ctx.enter_context(nc.semaphore("es_" + e)) for e in ENGS}
        self.waited = {}
        self.dma_sems = []
        self.dma_bufs = []
        self.free_sems = []

    def buf(self, name="b"):
        return Buf(name)

    def bufs(self, n, name="b"):
        return [Buf("%s%d" % (name, i)) for i in range(n)]

    def _ensure_sem(self, b):
        if b.sem is None:
            b.sem = self.ctx.enter_context(self.nc.semaphore("ds_%d" % len(self.dma_bufs)))
            self.dma_bufs.append(b)

    def _wait(self, eng, tok):
        if tok is None:
            return
        if tok[0] == "eng":
            X, n = tok[1], tok[2]
            if X == eng and eng in ("pe", "sp"):
                return
            key = (eng, X)
            if self.waited.get(key, 0) >= n:
                return
            self.waited[key] = n
            self.q[eng].append(("wait", self.esem[X], n))
        else:
            b, cnt = tok[1], tok[2]
            key = (eng, id(b))
            if self.waited.get(key, 0) >= cnt:
                return
            self.waited[key] = cnt
            self.q[eng].append(("wait", b.sem, 16 * cnt))

    def _deps(self, eng, reads, writes):
        best = {}

        def add(t):
            if t is None:
                return
            key = (t[0], t[1] if t[0] == "eng" else id(t[1]))
            if key not in best or best[key][2] < t[2]:
                best[key] = t
        for b in reads:
            add(b.last_w)
        for b in writes:
            add(b.last_w)
            for t in b.readers:
                add(t)
        for t in best.values():
            self._wait(eng, t)

    def _mark(self, tok, reads, writes):
        for b in writes:
            b.last_w = tok
            b.readers = []
        for b in reads:
            b.readers.append(tok)

    def op(self, eng, fn, reads=(), writes=()):
        self._deps(eng, reads, writes)
        self.seq[eng] += 1
        self.q[eng].append(("op", _record(fn)))
        self._mark(("eng", eng, self.seq[eng]), reads, writes)

    def dma(self, eng, fn, reads=(), writes=(), sem_buf=None):
        self._deps(eng, reads, writes)
        self._ensure_sem(sem_buf)
        sem_buf.dcount += 1
        self.q[eng].append(("dma", _record(fn), sem_buf.sem))
        self._mark(("dma", sem_buf, sem_buf.dcount), reads, writes)

    def dma_group(self, eng, fns, reads=(), writes=(), sem_buf=None):
        self._deps(eng, reads, writes)
        self._ensure_sem(sem_buf)
        for fn in fns:
            sem_buf.dcount += 1
            self.q[eng].append(("dma", _record(fn), sem_buf.sem))
        self._mark(("dma", sem_buf, sem_buf.dcount), reads, writes)

    def barrier(self):
        for e in ENGS:
            for x in ENGS:
                if x != e and self.seq[x] > 0:
                    self._wait(e, ("eng", x, self.seq[x]))
            for b in self.dma_bufs:
                if b.dcount > 0:
                    self._wait(e, ("dma", b, b.dcount))

    def emit(self):
        nc = self.nc
        engobj = {"pe": nc.tensor, "act": nc.scalar, "dve": nc.vector, "pool": nc.gpsimd, "sp": nc.sync}
        with nc.Block() as block:
            def run(ename):
                def f(_e):
                    e = engobj[ename]
                    sem = self.esem[ename]
                    for item in self.q[ename]:
                        if item[0] == "wait":
                            e.wait_ge(item[1], item[2])
                        elif item[0] == "op":
                            name, a, kw = item[1]
                            getattr(e, name)(*a, **kw).then_inc(sem, 1)
                        else:
                            name, a, kw = item[1]
                            getattr(e, name)(*a, **kw).then_inc(item[2], 16)
                return f
            reg = {"pe": block.tensor, "act": block.scalar, "dve": block.vector, "pool": block.gpsimd, "sp": block.sync}
            for en in ENGS:
                if self.q[en]:
                    reg[en](run(en))


def _bf(a):
    return np.ascontiguousarray(a.astype(np.float32)).astype(ml_dtypes.bfloat16)


def make_dft_tables():
    ch = np.arange(128)[:, None].astype(np.float64)
    m = np.arange(128)[None, :].astype(np.float64)
    ang = 2 * np.pi * ch * m / 128.0
    sc = 1.0 / math.sqrt(128.0)
    tabC = np.concatenate([np.cos(ang) * sc, -np.sin(ang) * sc], axis=1)
    r = np.arange(64)[:, None].astype(np.float64)
    ka = np.arange(64)[None, :].astype(np.float64)
    th = 2 * np.pi * r * ka / 64.0
    c, s = np.cos(th) / 8.0, np.sin(th) / 8.0
    R_re = np.concatenate([c, s], axis=0)
    R_im = np.concatenate([-s, c], axis=0)
    tabR = np.concatenate([R_re, R_im], axis=1)
    cc = np.arange(64)[:, None, None].astype(np.float64)
    kaa = np.arange(64)[None, :, None].astype(np.float64)
    kb = np.arange(64)[None, None, :].astype(np.float64)
    phi = 2 * np.pi * (cc * kb / 64.0 + cc * kaa / 4096.0)
    V = np.concatenate([np.cos(phi) / 8.0, np.sin(phi) / 8.0], axis=0)
    return _bf(tabC), _bf(tabR), _bf(V.reshape(128, 64 * 64))


def na_row_range(j):
    qlo = 0 if j <= 3 else 2 * j - 3
    qhi = 63 if j >= 28 else 2 * j + 5
    return qlo, qhi


def na_slot(j, r):
    t = r - 2 * j + 7
    if t == 4 and (j, r) in ((2, 1), (3, 3)):
        return 14
    if t == 12 and (j, r) in ((28, 61), (29, 63)):
        return 15
    assert 1 <= t <= 14
    return t - 1


def make_bias_table(rpb):
    kc = np.arange(64)[:, None]
    qc = np.arange(64)[None, :]
    cs = np.clip(qc - 8, 0, 48)
    col_in = (kc >= cs) & (kc < cs + 16)
    dc = np.clip(kc - qc + 15, 0, 30)
    tab = np.full((NH, 2, 64, NSLOT, 64), NEG, np.float32)
    for slot in range(NSLOT):
        t = slot + 1 if slot < 14 else (4 if slot == 14 else 12)
        for krb in range(2):
            dr = krb + 7 - t
            if abs(dr) > 7:
                continue
            if slot < 14 and ((krb == 1 and t == 4) or (krb == 0 and t == 12)):
                continue
            g = rpb[:, dr + 7, :][:, dc]
            tab[:, krb, :, slot, :] = np.where(col_in[None], g, np.float32(NEG))
    return tab.reshape(NH, 128, NSLOT * 64)


def _check_na_tables():
    rs = np.clip(np.arange(64) - 4, 0, 56)
    cover = np.zeros((64, 64), np.int32)
    for j in range(32):
        qlo, qhi = na_row_range(j)
        for r in range(qlo, qhi + 1):
            slot = na_slot(j, r)
            t = slot + 1 if slot < 14 else (4 if slot == 14 else 12)
            assert t == r - 2 * j + 7
            for krb in range(2):
                kr = 2 * j + krb
                valid_tab = not (slot < 14 and ((krb == 1 and t == 4) or (krb == 0 and t == 12)))
                valid = rs[r] <= kr <= rs[r] + 7
                assert valid == valid_tab, (j, r, krb)
                if valid:
                    cover[r, kr] += 1
    for r in range(64):
        assert cover[r].sum() == 8 and cover[r, rs[r]:rs[r] + 8].sum() == 8


class Builder:
    def __init__(self, phases=("ada", "attn", "ffn0", "four", "ffn1"), debug=False):
        self.phases = phases
        self.debug = debug
        self.nc = bass.Bass("TRN2", target_bir_lowering=False)
        self.dbg_outs = {}
        self.deferred = []
        self.bg_ada = []
        self.bg_pre = []

    def dram_in(self, name, shape, dt=F32):
        return self.nc.dram_tensor(name, list(shape), dt, kind="ExternalInput").ap()

    def build(self):
        nc = self.nc
        self.xT = self.dram_in("xT", [128, NCH, S])
        self.ccol = self.dram_in("ccol", [128, NCH])
        self.ada_w = self.dram_in("ada_w", [2, 128, NCH, 6 * D])
        self.ada_b = self.dram_in("ada_b", [2, 6 * D])
        self.wqkv = self.dram_in("wqkv", [NCH, 128, NCH * 3 * 128])
        self.tab = self.dram_in("tab", [NH, 128, NSLOT * 64])
        self.na_wo = self.dram_in("na_wo", [128, NCH, D])
        self.fn_wo = self.dram_in("fn_wo", [128, NCH, D])
        self.lnp = self.dram_in("lnp", [128, 4, 2, NCH])
        self.w_up = self.dram_in("w_up", [2, 128, NCH, 2 * DFF])
        self.w_dn = self.dram_in("w_dn", [2, 128, NF, D])
        self.convp = self.dram_in("convp", [2, 128, 4, NF])
        self.tabC = self.dram_in("tabC", [128, 256], BF16)
        self.tabR = self.dram_in("tabR", [128, 128], BF16)
        self.tabV = self.dram_in("tabV", [128, 4096], BF16)
        self.ident = self.dram_in("ident", [128, 128], BF16)
        self.out = nc.dram_tensor("outT", [128, NCH, S], F32, kind="ExternalOutput").ap()
        self.xs = [nc.dram_tensor("xs%d" % i, [128, NCH, S], F32, kind="Internal").ap() for i in range(3)]
        self.wup_bf = [nc.dram_tensor("wupbf%d" % i, [128, NCH, 2 * DFF], BF16, kind="Internal").ap() for i in range(2)]
        self.wdn_bf = [nc.dram_tensor("wdnbf%d" % i, [128, NF, D], BF16, kind="Internal").ap() for i in range(2)]
        self.wo_bf = [nc.dram_tensor("wobf%d" % i, [128, NCH, D], BF16, kind="Internal").ap() for i in range(2)]

        with ExitStack() as ctx:
            self.ctx = ctx
            self.P = P = Prog(nc, ctx)
            self.ps = ctx.enter_context(nc.psum_tensor("ps", [128, 4096], F32))
            self.pb = P.bufs(8, "psb")
            self.modp = self.sb(ctx, "modp", [128, 2, 48], F32)
            self.lnp_sb = self.sb(ctx, "lnp_sb", [128, 4 * 2 * NCH], F32)
            self.convp_sb = self.sb(ctx, "convp_sb", [128, 2, 4 * NF], F32)
            self.ones_bf = self.sb(ctx, "ones_bf", [128, 128], BF16)
            self.ident_bf = self.sb(ctx, "ident_bf", [128, 128], BF16)
            self.b_const = P.buf("const")
            self.b_modg = [P.bufs(6, "mod%d_" % l) for l in range(2)]
            self.b_xs = [P.bufs(8, "xs%d_" % i) for i in range(3)]
            self.b_out = P.bufs(16, "out")
            self.b_xin = P.bufs(8, "xin")

            P.dma("sp", lambda e: e.dma_start(out=self.lnp_sb[:], in_=self.lnp.rearrange("p a b c -> p (a b c)")),
                  writes=[self.b_const], sem_buf=self.b_const)
            for l in range(2):
                P.dma("sp", (lambda l: lambda e: e.dma_start(out=self.convp_sb[:, l, :], in_=self.convp[l].rearrange("p a f -> p (a f)")))(l),
                      writes=[self.b_const], sem_buf=self.b_const)
            P.dma("sp", lambda e: e.dma_start(out=self.ident_bf[:], in_=self.ident[:, :]), writes=[self.b_const], sem_buf=self.b_const)
            P.op("pool", lambda e: e.memset(self.ones_bf[:], 1.0), writes=[self.b_const])

            self.b_wupbf, self.b_wdnbf, self.b_wobf = P.bufs(2, "wupbf"), P.bufs(2, "wdnbf"), P.bufs(2, "wobf")
            self.make_precast()
            src, b_src = self.xT, self.b_xin
            with ExitStack() as actx:
                self.phase_ada(actx)
                if "attn" in self.phases:
                    dst = self.xs[0] if self.phases[-1] != "attn" else self.out
                    b_dst = self.b_xs[0] if self.phases[-1] != "attn" else self.b_out
                    self.phase_attn(src, b_src, dst, b_dst)
                    src, b_src = dst, b_dst
                self.drain_bg()
                P.barrier()
            if "ffn0" in self.phases:
                dst = self.xs[1] if self.phases[-1] != "ffn0" else self.out
                b_dst = self.b_xs[1] if self.phases[-1] != "ffn0" else self.b_out
                self.phase_ffn(0, src, b_src, dst, b_dst)
                src, b_src = dst, b_dst
            if "four" in self.phases:
                dst = self.xs[2] if self.phases[-1] != "four" else self.out
                b_dst = self.b_xs[2] if self.phases[-1] != "four" else self.b_out
                self.phase_four(src, b_src, dst, b_dst)
                src, b_src = dst, b_dst
            if "ffn1" in self.phases:
                self.phase_ffn(1, src, b_src, self.out, self.b_out)
            P.barrier()
            P.emit()
        return nc

    def sb(self, ctx, name, shape, dt):
        self._uid = getattr(self, "_uid", 0) + 1
        return ctx.enter_context(self.nc.sbuf_tensor("%s_%d" % (name, self._uid), list(shape), dt))

    def bank(self, b, c0=0, c1=512, p0=0, p1=128):
        return self.ps[p0:p1, b * 512 + c0: b * 512 + c1]

    def mod(self, l, which, k):
        base = {"sh1": 0, "sc1": 8, "g1": 16, "sh2": 24, "sc2": 32, "g2": 40}[which]
        return self.modp[:, l, base + k: base + k + 1]

    def bmod(self, l, which):
        return self.b_modg[l][{"sh1": 0, "sc1": 1, "g1": 2, "sh2": 3, "sc2": 4, "g2": 5}[which]]

    def lnv(self, idx, gb, k):
        o = (idx * 2 + gb) * NCH + k
        return self.lnp_sb[:, o:o + 1]

    def phase_ada(self, ctx):
        P, nc = self.P, self.nc
        c_sb = self.sb(ctx, "c_sb", [128, NCH], F32)
        cs_bf = self.sb(ctx, "cs_bf", [128, NCH], BF16)
        one_f = self.sb(ctx, "one_f", [1, 1], F32)
        stage = [self.sb(ctx, "ada_st%d" % i, [128, NCH, 256], BF16) for i in range(2)]
        brow = [self.sb(ctx, "ada_br%d" % i, [1, 256], F32) for i in range(2)]
        mrow = [self.sb(ctx, "ada_mr%d" % i, [1, 256], F32) for i in range(2)]
        b_c, b_cs = P.buf(), P.buf()
        b_st, b_br, b_mr = P.bufs(2, "ast"), P.bufs(2, "abr"), P.bufs(2, "amr")
        P.dma("sp", lambda e: e.dma_start(out=c_sb[:], in_=self.ccol[:, :]), writes=[b_c], sem_buf=b_c)
        P.op("act", lambda e: e.activation(out=cs_bf[:], in_=c_sb[:], func=AF.Silu), reads=[b_c], writes=[b_cs])
        P.op("pool", lambda e: e.memset(one_f[:], 1.0), writes=[b_cs])
        RB, TBK = 0, 1

        def dma(n):
            l, pc, slot = n // 24, n % 24, n % 2
            P.dma("pool", lambda e: e.dma_start(out=stage[slot][:], in_=self.ada_w[l, :, :, pc * 256:(pc + 1) * 256]),
                  writes=[b_st[slot]], sem_buf=b_st[slot])
            P.dma("sp", lambda e: e.dma_start(out=brow[slot][:], in_=self.ada_b[l:l + 1, pc * 256:(pc + 1) * 256]),
                  writes=[b_br[slot]], sem_buf=b_br[slot])

        pend = {"n": None}

        def trans(n):
            l, pc, slot = n // 24, n % 24, n % 2
            for jj in range(2):
                P.op("pe", lambda e: e.matmul(self.bank(TBK, jj, jj + 1), lhsT=mrow[slot][0:1, jj * 128:(jj + 1) * 128], rhs=one_f[0:1, 0:1],
                                              start=True, stop=True, skip_group_check=True),
                     reads=[b_mr[slot], b_cs], writes=[self.pb[TBK]])
            grp = (pc * 2) // 8
            add, mul = [(0.0, 1.0), (1.0, 1.0), (1.0, 1.0 / ALPHA)][grp % 3]
            P.op("dve", lambda e: e.tensor_scalar(out=self.modp[:, l, pc * 2:(pc + 1) * 2], in0=self.bank(TBK, 0, 2),
                                                  scalar1=add, scalar2=mul, op0=ALU.add, op1=ALU.mult),
                 writes=[self.pb[TBK], self.b_modg[l][grp]])

        def flush():
            if pend["n"] is not None:
                trans(pend["n"])
                pend["n"] = None

        def comp(n):
            l, pc, slot = n // 24, n % 24, n % 2
            for k in range(NCH):
                P.op("pe", lambda e: e.matmul(self.bank(RB, 0, 256, 0, 1), lhsT=cs_bf[:, k:k + 1], rhs=stage[slot][:, k, :],
                                              start=(k == 0), stop=(k == NCH - 1), skip_group_check=True),
                     reads=[b_st[slot], b_cs], writes=[self.pb[RB]])
            P.op("dve", lambda e: e.tensor_tensor(out=mrow[slot][:], in0=self.bank(RB, 0, 256, 0, 1), in1=brow[slot][:], op=ALU.add),
                 reads=[b_br[slot]], writes=[self.pb[RB], b_mr[slot]])
            flush()
            pend["n"] = n

        seq = []
        for n in range(48):
            seq.append(("d", n))
            if n >= 1:
                seq.append(("c", n - 1))
        seq.append(("c", 47))
        ncrit = 8
        for kind, n in seq:
            th = (lambda n=n: dma(n)) if kind == "d" else (lambda n=n: comp(n))
            if (kind == "d" and n < ncrit) or (kind == "c" and n < ncrit):
                th()
                if kind == "c" and n == ncrit - 1:
                    flush()
            else:
                self.bg_ada.append(th)
        self.bg_ada.append(flush)

    def stream_u(self, ph, src, b_src, l, sc, sh, uT, b_u):
        P = self.P
        xin = [self.sb(ph, "su_x%d" % i, [128, NCH, 512], F32) for i in range(3)]
        b_x = P.bufs(3, "su_x")
        self._stream_u_body(src, b_src, l, sc, sh, uT, b_u, xin, b_x)

    def _stream_u_body(self, src, b_src, l, sc, sh, uT, b_u, xin, b_x):
        P = self.P
        for tb in range(8):
            slot = tb % 3
            for hh in range(2):
                P.dma("sp", (lambda tb, slot, hh: lambda e: e.dma_start(
                    out=xin[slot][:, hh * 4:(hh + 1) * 4, :], in_=src[:, hh * 4:(hh + 1) * 4, tb * 512:(tb + 1) * 512]))(tb, slot, hh),
                    reads=[b_src[tb]], writes=[b_x[slot]], sem_buf=b_x[slot])
            for k in range(NCH):
                if k % 2 == 0:
                    P.op("act", (lambda tb, slot, k: lambda e: e.activation(
                        out=uT[:, k, tb * 512:(tb + 1) * 512], in_=xin[slot][:, k, :], func=AF.Identity,
                        scale=self.mod(l, sc, k), bias=self.mod(l, sh, k)))(tb, slot, k),
                        reads=[b_x[slot], self.bmod(l, sc), self.bmod(l, sh)], writes=[b_u[tb]])
                else:
                    P.op("dve", (lambda tb, slot, k: lambda e: e.tensor_scalar(
                        out=uT[:, k, tb * 512:(tb + 1) * 512], in0=xin[slot][:, k, :],
                        scalar1=self.mod(l, sc, k), scalar2=self.mod(l, sh, k), op0=ALU.mult, op1=ALU.add))(tb, slot, k),
                        reads=[b_x[slot], self.bmod(l, sc), self.bmod(l, sh)], writes=[b_u[tb]])

    def ln_pre(self, T, zk, bzk, r, tmp, b_tmp):
        P = self.P
        P.op("act", lambda e: e.activation(out=tmp["zb"][:, r, 0:T], in_=zk, func=AF.Copy), reads=[bzk], writes=[b_tmp["zb"][r]])
        P.op("act", lambda e: e.activation(out=tmp["zsq"][:, r, 0:T], in_=zk, func=AF.Square), reads=[bzk], writes=[b_tmp["zsq"][r]])

    def ln_stats_mm(self, T, k, r, tmp, b_tmp, s1, s2, same_bank):
        P = self.P
        P.op("pe", lambda e: e.matmul(self.bank(s1[0], s1[1], s1[1] + T), lhsT=self.ones_bf[:], rhs=tmp["zb"][:, r, 0:T],
                                      start=(k == 0), stop=(k == NCH - 1), skip_group_check=True),
             reads=[b_tmp["zb"][r], self.b_const], writes=[self.pb[s1[0]]])
        P.op("pe", lambda e: e.matmul(self.bank(s2[0], s2[1], s2[1] + T), lhsT=self.ones_bf[:], rhs=tmp["zsq"][:, r, 0:T],
                                      start=(k == 0 and not same_bank), stop=(k == NCH - 1), skip_group_check=True),
             reads=[b_tmp["zsq"][r], self.b_const], writes=[self.pb[s2[0]]])

    def ln_finish_groups(self, T, z, b_z, ln_idx, tmp, b_tmp, s1, s2, add_eng="dve"):
        P = self.P
        bs = b_tmp["st"]
        S1 = self.bank(s1[0], s1[1], s1[1] + T)
        S2 = self.bank(s2[0], s2[1], s2[1] + T)
        groups = []

        def g0():
            P.op("act", lambda e: e.activation(out=tmp["mean"][:, 0:T], in_=S1, func=AF.Copy, scale=1.0 / D),
                 writes=[self.pb[s1[0]], bs])
            P.op("act", lambda e: e.activation(out=tmp["msq"][:, 0:T], in_=S1, func=AF.Square, scale=1.0 / D),
                 writes=[self.pb[s1[0]], bs])

        def g1():
            P.op("dve", lambda e: e.scalar_tensor_tensor(out=tmp["var"][:, 0:T], in0=S2, scalar=1.0 / D, in1=tmp["msq"][:, 0:T],
                                                         op0=ALU.mult, op1=ALU.subtract),
                 reads=[bs], writes=[self.pb[s2[0]], bs])
            P.op("dve", lambda e: e.tensor_scalar(out=tmp["var"][:, 0:T], in0=tmp["var"][:, 0:T], scalar1=EPS_P, scalar2=None, op0=ALU.add),
                 reads=[bs], writes=[bs])

        def g2():
            P.op("act", lambda e: e.activation(out=tmp["var"][:, 0:T], in_=tmp["var"][:, 0:T], func=AF.Sqrt), reads=[bs], writes=[bs])

        def g3():
            P.op("dve", lambda e: e.reciprocal(out=tmp["rstd"][:, 0:T], in_=tmp["var"][:, 0:T]), reads=[bs], writes=[bs])
            P.op("dve", lambda e: e.scalar_tensor_tensor(out=tmp["nmr"][:, 0:T], in0=tmp["mean"][:, 0:T], scalar=-1.0, in1=tmp["rstd"][:, 0:T],
                                                         op0=ALU.mult, op1=ALU.mult),
                 reads=[bs], writes=[bs])
        groups += [g0, g1, g2, g3]

        def mk(i):
            def g():
                if i < NCH:
                    k = i
                    P.op("dve", lambda e: e.tensor_tensor(out=z(k), in0=z(k), in1=tmp["rstd"][:, 0:T], op=ALU.mult),
                         reads=[bs, b_z(k)], writes=[b_z(k)])
                if 0 <= i - 1 < NCH:
                    k = i - 1
                    P.op(add_eng, lambda e: e.tensor_tensor(out=z(k), in0=z(k), in1=tmp["nmr"][:, 0:T], op=ALU.add),
                         reads=[bs, b_z(k)], writes=[b_z(k)])
                if 0 <= i - 2 < NCH:
                    k = i - 2
                    P.op("act", lambda e: e.activation(out=z(k), in_=z(k), func=AF.Identity,
                                                       scale=self.lnv(ln_idx, 0, k), bias=self.lnv(ln_idx, 1, k)),
                         reads=[b_z(k), self.b_const], writes=[b_z(k)])
            return g
        groups += [mk(i) for i in range(NCH + 2)]
        return groups

    def make_precast(self):
        P = self.P

        def wo(i):
            src = self.na_wo if i == 0 else self.fn_wo
            for k in range(NCH):
                self.bg_pre.append(lambda k=k: P.dma("pool", lambda e: e.dma_start(out=self.wo_bf[i][:, k, :], in_=src[:, k, :]),
                                                     writes=[self.b_wobf[i]], sem_buf=self.b_wobf[i]))

        def ff(l):
            for k in range(NCH):
                for c0 in range(0, 2 * DFF, 1408):
                    self.bg_pre.append(lambda k=k, c0=c0: P.dma("pool", lambda e: e.dma_start(out=self.wup_bf[l][:, k, c0:c0 + 1408],
                                                                                              in_=self.w_up[l, :, k, c0:c0 + 1408]),
                                                                writes=[self.b_wupbf[l]], sem_buf=self.b_wupbf[l]))
            for f in range(0, NF, 2):
                self.bg_pre.append(lambda f=f: P.dma("pool", lambda e: e.dma_start(out=self.wdn_bf[l][:, f:f + 2, :], in_=self.w_dn[l, :, f:f + 2, :]),
                                                     writes=[self.b_wdnbf[l]], sem_buf=self.b_wdnbf[l]))
        wo(0)
        ff(0)
        wo(1)
        ff(1)

    def drain_bg(self, n_ada=None, n_pre=None):
        for q, n in ((self.bg_ada, n_ada), (self.bg_pre, n_pre)):
            n = len(q) if n is None else min(n, len(q))
            for _ in range(n):
                q.pop(0)()

    def drain(self, n=None):
        q = self.deferred
        n = len(q) if n is None else min(n, len(q))
        for _ in range(n):
            q.pop(0)()

    def alloc_ln_tmp(self, ph, T, tag):
        P = self.P
        tmp = {"zb": self.sb(ph, tag + "zb", [128, 4, T], BF16), "zsq": self.sb(ph, tag + "zsq", [128, 4, T], BF16)}
        for n in ("mean", "msq"):
            tmp[n] = self.sb(ph, tag + n, [128, T], F32)
        tmp["var"] = tmp["msq"]
        tmp["rstd"] = tmp["msq"]
        tmp["nmr"] = tmp["mean"]

        b_tmp = {"zb": P.bufs(4, "zb"), "zsq": P.bufs(4, "zsq"), "st": P.buf("st")}
        return tmp, b_tmp

    def proj_ln(self, ph, inT, b_in, w_dram, l, src, b_src, dst, b_dst, ln_idx):
        P = self.P
        w_sb = self.sb(ph, "pl_w", [128, NCH, D], BF16)
        b_w = P.buf("pl_w")
        self.drain_bg()
        wi = 0 if l == 0 else 1
        for hh in range(2):
            P.dma("sp", lambda e: e.dma_start(out=w_sb[:, hh * 4:(hh + 1) * 4, :], in_=self.wo_bf[wi][:, hh * 4:(hh + 1) * 4, :]),
                  reads=[self.b_wobf[wi]], writes=[b_w], sem_buf=b_w)
        NS = 3
        xr = [self.sb(ph, "pl_x%d" % i, [128, NCH, 512], F32) for i in range(NS)]
        b_xr = [P.bufs(NCH, "pl_x%d_" % i) for i in range(NS)]
        tmp, b_tmp = self.alloc_ln_tmp(ph, 512, "pl_")
        cnt = {"y": 0, "z": 0}

        sx = P.bufs(NS, "pl_sx")

        def load(tb):
            slot = tb % NS
            P.dma_group("sp", [(lambda e, hh=hh: e.dma_start(out=xr[slot][:, hh * 4:(hh + 1) * 4, :],
                                                             in_=src[:, hh * 4:(hh + 1) * 4, tb * 512:(tb + 1) * 512])) for hh in range(2)],
                        reads=[b_src[tb]], writes=b_xr[slot], sem_buf=sx[slot])

        def sbank(tb):
            sb0 = 3 + 2 * (tb % 2)
            return (sb0, 0), (sb0 + 1, 0)

        def A(tb):
            slot = tb % NS
            tok = slice(tb * 512, (tb + 1) * 512)
            s1, s2 = sbank(tb)
            prev = None
            for k in range(NCH):
                yb = cnt["y"] % 3
                cnt["y"] += 1
                for kk in range(NCH):
                    P.op("pe", lambda e: e.matmul(self.bank(yb), lhsT=w_sb[:, kk, k * 128:(k + 1) * 128], rhs=inT[:, kk, tok],
                                                  start=(kk == 0), stop=(kk == NCH - 1)),
                         reads=[b_w] + (b_in[tb] if isinstance(b_in[tb], list) else [b_in[tb]]), writes=[self.pb[yb]])
                P.op("dve", lambda e: e.scalar_tensor_tensor(out=xr[slot][:, k, :], in0=self.bank(yb), scalar=self.mod(l, "g1", k),
                                                             in1=xr[slot][:, k, :], op0=ALU.mult, op1=ALU.add),
                     reads=[self.bmod(l, "g1"), b_xr[slot][k]], writes=[self.pb[yb], b_xr[slot][k]])
                r = cnt["z"] % 4
                cnt["z"] += 1
                self.ln_pre(512, xr[slot][:, k, :], b_xr[slot][k], r, tmp, b_tmp)
                if prev is not None:
                    self.ln_stats_mm(512, prev[0], prev[1], tmp, b_tmp, s1, s2, False)
                prev = (k, r)
                self.drain(2)
            self.ln_stats_mm(512, prev[0], prev[1], tmp, b_tmp, s1, s2, False)

        def B(tb):
            slot = tb % NS
            s1, s2 = sbank(tb)
            grp = self.ln_finish_groups(512, lambda k: xr[slot][:, k, :], lambda k: b_xr[slot][k], ln_idx, tmp, b_tmp, s1, s2,
                                        add_eng="pool")
            for g in grp[:4]:
                g()
            self.deferred += grp[4:]

            def out():
                P.dma_group("sp", [(lambda e, hh=hh: e.dma_start(out=dst[:, hh * 4:(hh + 1) * 4, tb * 512:(tb + 1) * 512],
                                                                 in_=xr[slot][:, hh * 4:(hh + 1) * 4, :])) for hh in range(2)],
                            reads=b_xr[slot], writes=[b_dst[tb]], sem_buf=sx[slot])
            self.deferred.append(out)

        load(0)
        load(1)
        A(0)
        B(0)
        for tb in range(1, 8):
            A(tb)
            self.drain()
            if tb + 1 < 8:
                load(tb + 1)
            B(tb)
        self.drain()
    def phase_attn(self, src, b_src, dst, b_dst):
        P, nc = self.P, self.nc
        l = 0
        with ExitStack() as ph:
            attnT = self.sb(ph, "attnT", [128, NCH, S], BF16)
            b_at = P.bufs(8, "attnT")
            with ExitStack() as ph2:
                uT = self.sb(ph2, "uT", [128, NCH, S], BF16)
                b_u = P.bufs(8, "uT")
                with ExitStack() as ph3:
                    self.stream_u(ph3, src, b_src, l, "sc1", "sh1", uT, b_u)
                    P.barrier()
                wq = [self.sb(ph2, "wq%d" % i, [128, NCH, 3, 128], BF16) for i in range(2)]
                tabs = [self.sb(ph2, "tab%d" % i, [128, 2, NSLOT * 64], BF16) for i in range(2)]
                b_wq = P.bufs(2, "wq")
                b_tab = P.bufs(2, "tab")
                qT = self.sb(ph2, "qT", [128, S], BF16)
                kz = [self.sb(ph2, "kz%d" % i, [128, S], BF16) for i in range(2)]
                vt = self.sb(ph2, "vt", [128, 32, 128], BF16)
                PT = [self.sb(ph2, "PT%d" % i, [128, 768], BF16) for i in range(2)]
                rden = [self.sb(ph2, "rden%d" % i, [128, 256], F32) for i in range(2)]
                b_q, b_k, b_v = P.bufs(8, "q"), P.bufs(8, "k"), P.bufs(8, "v")
                b_PT = P.bufs(2, "PT")
                b_rd = P.bufs(2, "rden")
                b_kz0 = P.buf("kz0")
                P.op("pool", lambda e: e.memset(kz[0][64:128, :], 0.0), writes=[b_kz0])
                P.op("pool", lambda e: e.memset(kz[1][0:64, :], 0.0), writes=[b_kz0])
                evac_i = 0
                def load_hp(i):
                    ws = i % 2
                    P.dma_group("pool", [lambda e: e.dma_start(out=wq[ws].rearrange("p k t e -> p (k t e)")[:, 0:1536], in_=self.wqkv[i, :, 0:1536]),
                                         lambda e: e.dma_start(out=wq[ws].rearrange("p k t e -> p (k t e)")[:, 1536:3072], in_=self.wqkv[i, :, 1536:3072])],
                                writes=[b_wq[ws]], sem_buf=b_wq[ws])
                    P.dma_group("pool", [(lambda e, hp=hp: e.dma_start(out=tabs[ws][:, hp, :], in_=self.tab[2 * i + hp, :, :])) for hp in range(2)],
                                writes=[b_tab[ws]], sem_buf=b_tab[ws])
                load_hp(0)
                for i in range(NCH):
                    ws = i % 2
                    if i + 1 < NCH:
                        load_hp(i + 1)
                    for tb in range(8):
                        tok = slice(tb * 512, (tb + 1) * 512)
                        for t in range(2):
                            bk = evac_i % 4
                            evac_i += 1
                            for kk in range(NCH):
                                P.op("pe", (lambda kk, t, bk, tok=tok: lambda e: e.matmul(self.bank(bk), lhsT=wq[ws][:, kk, t, :], rhs=uT[:, kk, tok],
                                                                                          start=(kk == 0), stop=(kk == NCH - 1)))(kk, t, bk),
                                     reads=[b_wq[ws], b_u[tb]], writes=[self.pb[bk]])
                            if t == 0:
                                P.op("act", (lambda bk, tok=tok: lambda e: e.activation(out=qT[:, tok], in_=self.bank(bk), func=AF.Copy, scale=0.125))(bk),
                                     writes=[self.pb[bk], b_q[tb]])
                            else:
                                P.op("dve", (lambda bk, tok=tok: lambda e: e.tensor_copy(out=kz[0][0:64, tok], in_=self.bank(bk, 0, 512, 0, 64)))(bk),
                                     reads=[b_kz0], writes=[self.pb[bk], b_k[tb]])
                                P.op("act", (lambda bk, tok=tok: lambda e: e.activation(out=kz[1][64:128, tok], in_=self.bank(bk, 0, 512, 64, 128),
                                                                                         func=AF.Copy))(bk),
                                     reads=[b_kz0], writes=[self.pb[bk], b_k[tb]])
                        bk = evac_i % 4
                        evac_i += 1
                        for jj in range(4):
                            j = tb * 4 + jj
                            for kk in range(NCH):
                                P.op("pe", (lambda kk, jj, j, bk: lambda e: e.matmul(self.bank(bk, jj * 128, (jj + 1) * 128),
                                                                                     lhsT=uT[:, kk, j * 128:(j + 1) * 128], rhs=wq[ws][:, kk, 2, :],
                                                                                     start=(kk == 0), stop=(kk == NCH - 1)))(kk, jj, j, bk),
                                     reads=[b_wq[ws], b_u[tb]], writes=[self.pb[bk]])
                        P.op("dve", (lambda tb, bk: lambda e: e.tensor_copy(out=vt[:, tb * 4:(tb + 1) * 4, :].rearrange("p a b -> p (a b)"),
                                                                             in_=self.bank(bk)))(tb, bk),
                             writes=[self.pb[bk], b_v[tb]])
                    def qk(hp, j):
                        sbi = j % 2
                        bA, bB = 2 * sbi, 2 * sbi + 1
                        qlo, qhi = na_row_range(j)
                        n = qhi - qlo + 1
                        nA = min(n, 8)
                        nB = n - nA
                        kt = j // 4
                        for (bk, r0, nr) in ((bA, qlo, nA), (bB, qlo + 8, nB)):
                            if nr <= 0:
                                continue
                            qtbs = sorted(set([(r0 * 64) // 512, ((r0 + nr) * 64 - 1) // 512]))
                            P.op("pe", lambda e: e.matmul(self.bank(bk, 0, nr * 64), lhsT=kz[hp][:, j * 128:(j + 1) * 128],
                                                          rhs=qT[:, r0 * 64:(r0 + nr) * 64], start=True, stop=False, skip_group_check=True),
                                 reads=[b_k[kt]] + [b_q[x] for x in qtbs], writes=[self.pb[bk]])
                            runs = []
                            for r in range(r0, r0 + nr):
                                sl = na_slot(j, r)
                                if runs and runs[-1][1] + runs[-1][2] == sl and sl < 14:
                                    runs[-1][2] += 1
                                else:
                                    runs.append([r, sl, 1])
                            for ri, (r, sl, cnt) in enumerate(runs):
                                P.op("pe", lambda e: e.matmul(self.bank(bk, (r - r0) * 64, (r - r0 + cnt) * 64), lhsT=self.ident_bf[:],
                                                              rhs=tabs[ws][:, hp, sl * 64:(sl + cnt) * 64], start=False,
                                                              stop=(ri == len(runs) - 1), skip_group_check=True),
                                     reads=[b_tab[ws], self.b_const], writes=[self.pb[bk]])
                            P.op("act", lambda e: e.activation(out=PT[sbi][:, (r0 - qlo) * 64:(r0 - qlo + nr) * 64],
                                                               in_=self.bank(bk, 0, nr * 64), func=AF.Exp),
                                 writes=[self.pb[bk], b_PT[sbi]])

                    def pv(hp, j):
                        sbi = j % 2
                        qlo, qhi = na_row_range(j)
                        kt = j // 4
                        for b in range(qlo // 4, qhi // 4 + 1):
                            r0 = max(qlo, 4 * b)
                            r1 = min(qhi, 4 * b + 3)
                            ob = 4 + (b % 4)
                            first = (j == max(0, 2 * b - 2))
                            lastj = (j == min(31, 2 * b + 3))
                            pc0, pc1 = (r0 - qlo) * 64, (r1 - qlo + 1) * 64
                            oc0, oc1 = (r0 - 4 * b) * 64, (r1 - 4 * b + 1) * 64
                            P.op("pe", lambda e: e.matmul(self.bank(ob, oc0, oc1), lhsT=vt[:, j, :], rhs=PT[sbi][:, pc0:pc1], start=first, stop=False,
                                                          skip_group_check=True),
                                 reads=[b_v[kt], b_PT[sbi]], writes=[self.pb[ob]])
                            P.op("pe", lambda e: e.matmul(self.bank(ob, 256 + oc0, 256 + oc1), lhsT=self.ones_bf[:], rhs=PT[sbi][:, pc0:pc1],
                                                          start=False, stop=lastj, skip_group_check=True),
                                 reads=[self.b_const, b_PT[sbi]], writes=[self.pb[ob]])
                            if lastj:
                                rs_ = b % 2
                                p0, p1 = hp * 64, hp * 64 + 64
                                P.op("dve", lambda e: e.reciprocal(out=rden[rs_][p0:p1, :], in_=self.bank(ob, 256, 512, p0, p1)),
                                     writes=[self.pb[ob], b_rd[rs_]])
                                P.op("dve", lambda e: e.tensor_tensor(out=attnT[p0:p1, i, b * 256:(b + 1) * 256], in0=self.bank(ob, 0, 256, p0, p1),
                                                                      in1=rden[rs_][p0:p1, :], op=ALU.mult),
                                     reads=[b_rd[rs_]], writes=[self.pb[ob], b_at[b // 2]])

                    steps = [(hp, j) for hp in range(2) for j in range(32)]
                    qk(*steps[0])
                    for si, (hp, j) in enumerate(steps):
                        step = i * 64 + si
                        if step % 4 == 3:
                            self.drain_bg(1, 0)
                        if step % 4 == 1:
                            self.drain_bg(0, 1)
                        if si + 1 < len(steps):
                            qk(*steps[si + 1])
                        pv(hp, j)
                self.drain_bg()
                P.barrier()
            self.proj_ln(ph, attnT, b_at, self.na_wo, l, src, b_src, dst, b_dst, 0)
            P.barrier()

    def phase_four(self, src, b_src, dst, b_dst):
        P, nc = self.P, self.nc
        l = 1
        with ExitStack() as ph:
            uT = self.sb(ph, "uyT", [128, NCH, S], BF16)
            yT = uT
            b_u = P.bufs(8, "uTc")
            with ExitStack() as ph2:
                xin = [self.sb(ph2, "fx%d" % i, [128, S], F32) for i in range(2)]
                b_x = P.bufs(2, "fx")

                def stream_chunk(k):
                    slot = k % 2
                    P.dma("sp", lambda e: e.dma_start(out=xin[slot][:], in_=src[:, k, :]), reads=list(b_src), writes=[b_x[slot]], sem_buf=b_x[slot])
                    for pc in range(4):
                        cs_ = slice(pc * 1024, (pc + 1) * 1024)
                        if pc % 2 == 0:
                            P.op("act", lambda e: e.activation(out=uT[:, k, cs_], in_=xin[slot][:, cs_], func=AF.Identity,
                                                               scale=self.mod(l, "sc1", k), bias=self.mod(l, "sh1", k)),
                                 reads=[b_x[slot], self.bmod(l, "sc1"), self.bmod(l, "sh1")], writes=[b_u[k]])
                        else:
                            P.op("dve", lambda e: e.tensor_scalar(out=uT[:, k, cs_], in0=xin[slot][:, cs_],
                                                                  scalar1=self.mod(l, "sc1", k), scalar2=self.mod(l, "sh1", k),
                                                                  op0=ALU.mult, op1=ALU.add),
                                 reads=[b_x[slot], self.bmod(l, "sc1"), self.bmod(l, "sh1")], writes=[b_u[k]])
                tC = self.sb(ph2, "tC", [128, 256], BF16)
                tR = self.sb(ph2, "tR", [128, 128], BF16)
                tV = self.sb(ph2, "tV", [128, 64, 64], BF16)
                b_t = P.buf("ftab")
                P.dma("sp", lambda e: e.dma_start(out=tC[:], in_=self.tabC[:, :]), writes=[b_t], sem_buf=b_t)
                P.dma("sp", lambda e: e.dma_start(out=tR[:], in_=self.tabR[:, :]), writes=[b_t], sem_buf=b_t)
                P.dma("sp", lambda e: e.dma_start(out=tV.rearrange("p a b -> p (a b)"), in_=self.tabV[:, :]), writes=[b_t], sem_buf=b_t)
                XCs = [self.sb(ph2, "XC%d" % i, [128, 64, 128], BF16) for i in range(2)]
                As = [self.sb(ph2, "A%d" % i, [128, 128, 64], BF16) for i in range(2)]
                b_XCs = [P.bufs(16, "XC%d_" % i) for i in range(2)]
                b_As = [P.bufs(16, "A%d_" % i) for i in range(2)]
                ev = {"n": 0}

                def nextbank():
                    bk = ev["n"] % 8
                    ev["n"] += 1
                    return bk

                def evac(out_ap, in_ap, writes):
                    if ev["n"] % 2 == 0:
                        P.op("act", lambda e: e.activation(out=out_ap, in_=in_ap, func=AF.Copy), writes=writes)
                    else:
                        P.op("dve", lambda e: e.tensor_copy(out=out_ap, in_=in_ap), writes=writes)

                def stageC(g):
                    XC, b_XC = XCs[g % 2], b_XCs[g % 2]
                    ug = uT[:, g, :].rearrange("p (r c) -> p c r", c=64)
                    for cq in range(16):
                        bk = nextbank()
                        for cc in range(4):
                            c = cq * 4 + cc
                            P.op("pe", lambda e: e.matmul(self.bank(bk, cc * 128, (cc + 1) * 128, 0, 64), lhsT=ug[:, c, :], rhs=tC[:, 0:128],
                                                          start=True, stop=True, skip_group_check=True),
                                 reads=[b_u[g], b_t], writes=[self.pb[bk]])
                            P.op("pe", lambda e: e.matmul(self.bank(bk, cc * 128, (cc + 1) * 128, 64, 128), lhsT=ug[:, c, :], rhs=tC[:, 128:256],
                                                          start=True, stop=True, skip_group_check=True),
                                 reads=[b_u[g], b_t], writes=[self.pb[bk]])
                        evac(XC[:, cq * 4:(cq + 1) * 4, :].rearrange("p a b -> p (a b)"), self.bank(bk), [self.pb[bk], b_XC[cq]])

                def stageR(g):
                    XC, b_XC = XCs[g % 2], b_XCs[g % 2]
                    A, b_A = As[g % 2], b_As[g % 2]
                    for mq in range(16):
                        bk = nextbank()
                        for mm in range(8):
                            m = mq * 8 + mm
                            P.op("pe", lambda e: e.matmul(self.bank(bk, mm * 64, (mm + 1) * 64, 0, 64), lhsT=XC[:, :, m], rhs=tR[:, 0:64],
                                                          start=True, stop=True, skip_group_check=True),
                                 reads=b_XC + [b_t], writes=[self.pb[bk]])
                            P.op("pe", lambda e: e.matmul(self.bank(bk, mm * 64, (mm + 1) * 64, 64, 128), lhsT=XC[:, :, m], rhs=tR[:, 64:128],
                                                          start=True, stop=True, skip_group_check=True),
                                 reads=b_XC + [b_t], writes=[self.pb[bk]])
                        evac(A[:, mq * 8:(mq + 1) * 8, :].rearrange("p a b -> p (a b)"), self.bank(bk), [self.pb[bk], b_A[mq]])

                def stageV(g):
                    A, b_A = As[g % 2], b_As[g % 2]
                    for kq in range(8):
                        bk = nextbank()
                        for kk in range(8):
                            ka = kq * 8 + kk
                            P.op("pe", lambda e: e.matmul(self.bank(bk, kk * 64, (kk + 1) * 64), lhsT=A[:, :, ka], rhs=tV[:, ka, :],
                                                          start=True, stop=True, skip_group_check=True),
                                 reads=b_A + [b_t], writes=[self.pb[bk]])
                        outv = yT[:, g, :].rearrange("p (kb ka) -> p kb ka", ka=64)[:, :, kq * 8:(kq + 1) * 8]
                        inv = self.bank(bk).rearrange("p (kk kb) -> p kb kk", kb=64)
                        evac(outv, inv, [self.pb[bk], b_u[g]])

                stream_chunk(0)
                stream_chunk(1)
                stageC(0)
                stream_chunk(2)
                stageC(1)
                stageR(0)
                for g in range(NCH):
                    if g + 3 < NCH:
                        stream_chunk(g + 3)
                    if g + 2 < NCH:
                        stageC(g + 2)
                    if g + 1 < NCH:
                        stageR(g + 1)
                    stageV(g)
                P.barrier()
            self.proj_ln(ph, yT, [list(b_u)] * 8, self.fn_wo, l, src, b_src, dst, b_dst, 2)
            P.barrier()

    def phase_ffn(self, l, src, b_src, dst, b_dst):
        P, nc = self.P, self.nc
        T = 256
        NT = S // T
        ln_idx = 2 * l + 1
        with ExitStack() as ph:
            wup = self.sb(ph, "wup", [128, NCH, 2 * DFF], BF16)
            wdn = self.sb(ph, "wdn", [128, NF, D], BF16)
            self.drain_bg()
            WG = [(0, 4), (4, 12), (12, 22)]
            DG = [(0, 8), (8, 22)]
            b_wupg, b_wdng = P.bufs(len(WG), "wup"), P.bufs(len(DG), "wdn")
            wg_of = lambda fc: [i for i, (a, b) in enumerate(WG) if a <= fc < b][0]
            dg_of = lambda fc: [i for i, (a, b) in enumerate(DG) if a <= fc < b][0]

            def load_weights():
                for g, (f0, f1) in enumerate(WG):
                    fns = []
                    for base in (0, DFF):
                        for q0 in range(f0, f1, 4):
                            q1 = min(q0 + 4, f1)
                            fns.append(lambda e, base=base, q0=q0, q1=q1: e.dma_start(out=wup[:, :, base + q0 * 128:base + q1 * 128],
                                                                                      in_=self.wup_bf[l][:, :, base + q0 * 128:base + q1 * 128]))
                    P.dma_group("sp", fns, reads=[self.b_wupbf[l]], writes=[b_wupg[g]], sem_buf=b_wupg[g])
                for g, (f0, f1) in enumerate(DG):
                    fns = [(lambda e, q0=q0: e.dma_start(out=wdn[:, q0:min(q0 + 4, f1), :], in_=self.wdn_bf[l][:, q0:min(q0 + 4, f1), :]))
                           for q0 in range(f0, f1, 4)]
                    P.dma_group("sp", fns, reads=[self.b_wdnbf[l]], writes=[b_wdng[g]], sem_buf=b_wdng[g])
            NS = 3
            xh = [self.sb(ph, "ff_x%d" % i, [128, NCH, T + 2], F32) for i in range(NS)]
            u2 = [self.sb(ph, "ff_u%d" % i, [128, NCH, T + 2], BF16) for i in range(2)]
            hh_ = [self.sb(ph, "ff_h%d" % i, [128, NF, T], BF16) for i in range(2)]
            a1 = [self.sb(ph, "ff_a%d" % i, [128, T], F32) for i in range(3)]
            ge = [self.sb(ph, "ff_g%d" % i, [128, T], BF16) for i in range(3)]
            gs = [self.sb(ph, "ff_gs%d" % i, [128, T], BF16) for i in range(3)]
            b_xh = [P.bufs(NCH, "ffx%d_" % i) for i in range(NS)]
            sxh = P.bufs(NS, "ff_sx")
            b_u2, b_h = P.bufs(2, "ffu"), P.bufs(2, "ffh")
            b_a1, b_ge, b_gs = P.bufs(3, "ffa"), P.bufs(3, "ffg"), P.bufs(3, "ffgs")
            tmp, b_tmp = self.alloc_ln_tmp(ph, T, "ff_")
            cw = lambda a, fc: self.convp_sb[:, l, a * NF + fc: a * NF + fc + 1]
            cnt = {"z": 0}

            def load_u(t):
                slot = t % NS
                us = t % 2
                s0 = t * T
                lo = max(s0 - 1, 0)
                hi = min(s0 + T + 1, S)
                c_lo = lo - (s0 - 1)
                w = hi - lo
                tb_set = sorted(set([lo // (S // len(b_src)), (hi - 1) // (S // len(b_src))]))
                P.dma_group("sp", [(lambda e, hh=hh: e.dma_start(out=xh[slot][:, hh * 4:(hh + 1) * 4, c_lo:c_lo + w],
                                                                 in_=src[:, hh * 4:(hh + 1) * 4, lo:hi])) for hh in range(2)],
                            reads=[b_src[x] for x in tb_set], writes=b_xh[slot], sem_buf=sxh[slot])
                for k in range(NCH):
                    P.op("pool", lambda e: e.tensor_scalar(out=u2[us][:, k, c_lo:c_lo + w], in0=xh[slot][:, k, c_lo:c_lo + w],
                                                           scalar1=self.mod(l, "sc2", k), scalar2=self.mod(l, "sh2", k),
                                                           op0=ALU.mult, op1=ALU.add),
                         reads=[b_xh[slot][k], self.bmod(l, "sc2"), self.bmod(l, "sh2")], writes=[b_u2[us]])
                if t == 0:
                    P.op("pool", lambda e: e.memset(u2[us][:, :, 0:1], 0.0), writes=[b_u2[us]])
                if t == NT - 1:
                    P.op("pool", lambda e: e.memset(u2[us][:, :, T + 1:T + 2], 0.0), writes=[b_u2[us]])

            def up(t):
                slot = t % NS
                us = t % 2

                def tail(fc):
                    r = fc % 3
                    P.op("act", lambda e: e.activation(out=ge[r][:], in_=a1[r][:], func=AF.Gelu_apprx_tanh), reads=[b_a1[r]], writes=[b_ge[r]])
                    P.op("pool", lambda e: e.tensor_tensor(out=hh_[us][:, fc, :], in0=ge[r][:], in1=gs[r][:], op=ALU.mult),
                         reads=[b_ge[r], b_gs[r]], writes=[b_h[us]])
                for fc in range(NF):
                    ab = fc % 3
                    gb = 3 + fc % 2
                    r = fc % 3
                    for k in range(NCH):
                        P.op("pe", lambda e: e.matmul(self.bank(ab, 0, T + 2), lhsT=wup[:, k, fc * 128:(fc + 1) * 128], rhs=u2[us][:, k, :],
                                                      start=(k == 0), stop=(k == NCH - 1)),
                             reads=[b_wupg[wg_of(fc)], b_u2[us]], writes=[self.pb[ab]])
                    for k in range(NCH):
                        P.op("pe", lambda e: e.matmul(self.bank(gb, 0, T), lhsT=wup[:, k, DFF + fc * 128:DFF + (fc + 1) * 128],
                                                      rhs=u2[us][:, k, 1:T + 1], start=(k == 0), stop=(k == NCH - 1)),
                             reads=[b_wupg[wg_of(fc)], b_u2[us]], writes=[self.pb[gb]])
                    P.op("act", lambda e: e.activation(out=a1[r][:], in_=self.bank(ab, 1, T + 1), func=AF.Identity,
                                                       scale=cw(1, fc), bias=cw(3, fc)),
                         reads=[self.b_const], writes=[self.pb[ab], b_a1[r]])
                    P.op("act", lambda e: e.activation(out=gs[r][:], in_=self.bank(gb, 0, T), func=AF.Copy),
                         writes=[self.pb[gb], b_gs[r]])
                    P.op("dve", lambda e: e.scalar_tensor_tensor(out=a1[r][:], in0=self.bank(ab, 0, T), scalar=cw(0, fc), in1=a1[r][:],
                                                                 op0=ALU.mult, op1=ALU.add),
                         reads=[self.b_const, b_a1[r]], writes=[self.pb[ab], b_a1[r]])
                    P.op("dve", lambda e: e.scalar_tensor_tensor(out=a1[r][:], in0=self.bank(ab, 2, T + 2), scalar=cw(2, fc), in1=a1[r][:],
                                                                 op0=ALU.mult, op1=ALU.add),
                         reads=[self.b_const, b_a1[r]], writes=[self.pb[ab], b_a1[r]])
                    if fc >= 1:
                        tail(fc - 1)
                    if fc in (0, 2, 4, 7) or fc >= 9:
                        self.drain(1)
                tail(NF - 1)

            def down_and_ln(t):
                slot = t % NS
                us = t % 2
                s0 = t * T
                s1, s2 = (7, 0), (7, T)
                pend = []
                for p in range(4):
                    yb = 5 + p % 2
                    ks = (2 * p, 2 * p + 1)
                    for k in ks:
                        c0 = (k % 2) * T
                        for fc in range(NF):
                            P.op("pe", lambda e: e.matmul(self.bank(yb, c0, c0 + T), lhsT=wdn[:, fc, k * 128:(k + 1) * 128], rhs=hh_[us][:, fc, :],
                                                          start=(fc == 0), stop=(fc == NF - 1), skip_group_check=True),
                                 reads=[b_wdng[dg_of(fc)], b_h[us]], writes=[self.pb[yb]])
                    newp = []
                    for k in ks:
                        c0 = (k % 2) * T
                        P.op("dve", lambda e: e.scalar_tensor_tensor(out=xh[slot][:, k, 1:T + 1], in0=self.bank(yb, c0, c0 + T),
                                                                     scalar=self.mod(l, "g2", k), in1=xh[slot][:, k, 1:T + 1],
                                                                     op0=ALU.mult, op1=ALU.add),
                             reads=[self.bmod(l, "g2"), b_xh[slot][k]], writes=[self.pb[yb], b_xh[slot][k]])
                        r = cnt["z"] % 4
                        cnt["z"] += 1
                        self.ln_pre(T, xh[slot][:, k, 1:T + 1], b_xh[slot][k], r, tmp, b_tmp)
                        newp.append((k, r))
                    for (k, r) in pend:
                        self.ln_stats_mm(T, k, r, tmp, b_tmp, s1, s2, True)
                    pend = newp
                for (k, r) in pend:
                    self.ln_stats_mm(T, k, r, tmp, b_tmp, s1, s2, True)
                self.deferred += self.ln_finish_groups(T, lambda k: xh[slot][:, k, 1:T + 1], lambda k: b_xh[slot][k], ln_idx, tmp, b_tmp, s1, s2)
                tbd = s0 // (S // len(b_dst))

                def out():
                    P.dma_group("sp", [(lambda e, hh=hh: e.dma_start(out=dst[:, hh * 4:(hh + 1) * 4, s0:s0 + T],
                                                                     in_=xh[slot][:, hh * 4:(hh + 1) * 4, 1:T + 1])) for hh in range(2)],
                                reads=b_xh[slot], writes=[b_dst[tbd]], sem_buf=sxh[slot])
                self.deferred.append(out)

            load_u(0)
            load_weights()
            up(0)
            load_u(1)
            for t in range(NT):
                if t + 1 < NT:
                    up(t + 1)
                self.drain()
                if t + 2 < NT:
                    load_u(t + 2)
                down_and_ln(t)
            self.drain()
            P.barrier()
def prep_inputs(x, c, ada_w, ada_b, na_w_qkv, na_rpb, na_w_o, fn_w_o, ln1_g, ln1_b,
                ffn_w_up, ffn_conv_w, ffn_conv_b, ffn_w_down, ln2_g, ln2_b):
    f = lambda a: np.ascontiguousarray(np.asarray(a, dtype=np.float32))
    x, c = f(x), f(c)
    B = x.shape[0]
    tabC, tabR, tabV = make_dft_tables()
    chunk = lambda w: f(w.reshape(NCH, 128, -1).transpose(1, 0, 2))
    common = {
        "ada_w": f(np.stack([chunk(f(ada_w)[l]) for l in range(2)])),
        "ada_b": f(ada_b),
        "wqkv": f(f(na_w_qkv)[0].reshape(NCH, 128, 3, NCH, 128).transpose(3, 1, 0, 2, 4).reshape(NCH, 128, NCH * 3 * 128)),
        "tab": f(make_bias_table(f(na_rpb)[0])),
        "na_wo": chunk(f(na_w_o)[0]),
        "fn_wo": chunk(f(fn_w_o)[0]),
        "lnp": f(np.stack([np.stack([f(g)[l].reshape(NCH, 128).T, f(b)[l].reshape(NCH, 128).T], axis=1)
                           for l in range(2) for (g, b) in ((ln1_g, ln1_b), (ln2_g, ln2_b))], axis=1)),
        "w_up": f(np.stack([chunk(f(ffn_w_up)[l]) for l in range(2)])),
        "w_dn": f(np.stack([f(ffn_w_down)[l].reshape(NF, 128, D).transpose(1, 0, 2) for l in range(2)])),
        "convp": f(np.stack([np.concatenate([f(ffn_conv_w)[l], f(ffn_conv_b)[l][None]], axis=0).reshape(4, NF, 128).transpose(2, 0, 1)
                             for l in range(2)])),
        "tabC": tabC, "tabR": tabR, "tabV": tabV,
        "ident": _bf(np.eye(128)),
    }
    in_maps = []
    for b in range(B):
        m = dict(common)
        m["xT"] = f(x[b].T.reshape(NCH, 128, S).transpose(1, 0, 2))
        m["ccol"] = f(c[b].reshape(NCH, 128).T)
        in_maps.append(m)
    return in_maps


_NC_CACHE = {}


def get_nc(phases=("ada", "attn", "ffn0", "four", "ffn1")):
    key = tuple(phases)
    if key not in _NC_CACHE:
        _NC_CACHE[key] = Builder(phases).build()
    return _NC_CACHE[key]


def kernel(x, c, ada_w, ada_b, na_w_qkv, na_rpb, na_w_o, fn_w_o, ln1_g, ln1_b,
           ffn_w_up, ffn_conv_w, ffn_conv_b, ffn_w_down, ln2_g, ln2_b):
    in_maps = prep_inputs(x, c, ada_w, ada_b, na_w_qkv, na_rpb, na_w_o, fn_w_o, ln1_g, ln1_b,
                          ffn_w_up, ffn_conv_w, ffn_conv_b, ffn_w_down, ln2_g, ln2_b)
    nc = get_nc()
    res = run_bass_kernel_spmd(nc, in_maps, core_ids=list(range(len(in_maps))))
    outs = [np.asarray(r["outT"]).transpose(1, 0, 2).reshape(D, S).T for r in res.results]
    return np.ascontiguousarray(np.stack(outs, axis=0).astype(np.float32))
```
